# Optimizing a Trainium2 kernel written in Bass

```python
import jax, jax.numpy as jnp
from jax import lax
import numpy as np

D_MODEL = 1024
BATCH = 4
SEQ = 8192
DEPTH = 4

CHUNK = 64
N_A_LAYERS = DEPTH // 2
N_B_LAYERS = DEPTH - N_A_LAYERS
EXPAND = 2
D_INNER = EXPAND * D_MODEL
POOL_WINDOWS = (2, 4, 8, 16)
N_POOL_GROUPS = len(POOL_WINDOWS)
POOL_GROUP_W = D_INNER // N_POOL_GROUPS
N_HEADS = 16
HEAD_DIM = D_INNER // N_HEADS
LEFT_CHUNKS = 8
BAND = (LEFT_CHUNKS + 1) * CHUNK
REL_CLIP = 128
EPS = 1e-6

kernel_name = "yoco_pool_chunkattn_adaln_trunk"


def rms_norm(x, g):
    xf = x.astype(jnp.float32)
    y = xf * lax.rsqrt(jnp.mean(xf * xf, axis=-1, keepdims=True) + EPS)
    return (y * g.astype(jnp.float32)).astype(x.dtype)


def modulate(h, shift, scale):
    return h * (1 + scale[:, None, :]) + shift[:, None, :]


def multiscale_pool(v):
    B, S, _ = v.shape
    vf = v.astype(jnp.float32).reshape(B, S, N_POOL_GROUPS, POOL_GROUP_W)
    cs = jnp.cumsum(vf, axis=1)
    t = jnp.arange(S)
    outs = []
    for g, w in enumerate(POOL_WINDOWS):
        csg = cs[:, :, g]
        lagged = jnp.pad(csg, ((0, 0), (w, 0), (0, 0)))[:, :S]
        cnt = jnp.minimum(t + 1, w).astype(jnp.float32)[None, :, None]
        outs.append((csg - lagged) / cnt - vf[:, :, g])
    return jnp.stack(outs, axis=2)


def chunk_band_attention(q, k, v, rel_bias):
    B, S, H, Dh = q.shape
    n_chunks = S // CHUNK
    pad = LEFT_CHUNKS * CHUNK
    k_pad = jnp.pad(k, ((0, 0), (pad, 0), (0, 0), (0, 0)))
    v_pad = jnp.pad(v, ((0, 0), (pad, 0), (0, 0), (0, 0)))
    qi = jnp.arange(CHUNK)[:, None]
    kj = jnp.arange(BAND)[None, :]
    dist = pad + qi - kj
    idx = jnp.clip(dist, -REL_CLIP, REL_CLIP) + REL_CLIP
    bias = rel_bias[:, idx].astype(jnp.float32)
    sm_scale = HEAD_DIM ** -0.5
    band_pos = jnp.arange(BAND) - pad

    def one_chunk(n):
        start = n * CHUNK
        qb = lax.dynamic_slice_in_dim(q, start, CHUNK, axis=1)
        kb = lax.dynamic_slice_in_dim(k_pad, start, BAND, axis=1)
        vb = lax.dynamic_slice_in_dim(v_pad, start, BAND, axis=1)
        s = jnp.einsum('bqhd,bkhd->bhqk', qb, kb).astype(jnp.float32) * sm_scale + bias[None]
        valid = (start + band_pos) >= 0
        s = jnp.where(valid[None, None, None, :], s, -jnp.inf)
        p = jax.nn.softmax(s, axis=-1)
        return jnp.einsum('bhqk,bkhd->bqhd', p.astype(vb.dtype), vb)

    out = lax.map(one_chunk, jnp.arange(n_chunks))
    return out.transpose(1, 0, 2, 3, 4).reshape(B, S, H * Dh)


def setup_inputs(seed: int = 0) -> dict:
    key = jax.random.key(seed)
    ks = jax.random.split(key, 20)
    D, E, Gw = D_MODEL, D_INNER, POOL_GROUP_W
    nrm = jax.random.normal
    return {
        "x": nrm(ks[0], (BATCH, SEQ, D), jnp.float32),
        "c": nrm(ks[1], (BATCH, D), jnp.float32),
        "ada_w": nrm(ks[2], (DEPTH, D, 3 * D), jnp.float32) * D ** -0.5,
        "ada_b": 0.01 * nrm(ks[3], (DEPTH, 3 * D), jnp.float32),
        "norm_g": 1.0 + 0.05 * nrm(ks[4], (DEPTH, D), jnp.float32),
        "a_w_in": nrm(ks[5], (N_A_LAYERS, D, 2 * E), jnp.float32) * D ** -0.5,
        "a_w_group": nrm(ks[6], (N_A_LAYERS, N_POOL_GROUPS, Gw, Gw), jnp.float32) * Gw ** -0.5,
        "a_scale": 1.0 + 0.1 * nrm(ks[7], (N_A_LAYERS, E), jnp.float32),
        "a_w_out": nrm(ks[8], (N_A_LAYERS, E, D), jnp.float32) * E ** -0.5,
        "kv_norm_g": 1.0 + 0.05 * nrm(ks[9], (D,), jnp.float32),
        "kv_ada_w": nrm(ks[10], (D, 2 * D), jnp.float32) * D ** -0.5,
        "kv_ada_b": 0.01 * nrm(ks[11], (2 * D,), jnp.float32),
        "w_kv": nrm(ks[12], (D, 2 * E), jnp.float32) * D ** -0.5,
        "b_w_in": nrm(ks[13], (N_B_LAYERS, D, 2 * E), jnp.float32) * D ** -0.5,
        "b_rel_bias": 0.5 * nrm(ks[14], (N_B_LAYERS, N_HEADS, 2 * REL_CLIP + 1), jnp.float32),
        "b_w_out": nrm(ks[15], (N_B_LAYERS, E, D), jnp.float32) * E ** -0.5,
        "final_g": 1.0 + 0.05 * nrm(ks[16], (D,), jnp.float32),
    }


def reference(x, c, ada_w, ada_b, norm_g, a_w_in, a_w_group, a_scale, a_w_out,
              kv_norm_g, kv_ada_w, kv_ada_b, w_kv, b_w_in, b_rel_bias, b_w_out,
              final_g):
    B, S, _ = x.shape
    c_act = jax.nn.silu(c)
    h = x
    k = v = None
    for layer in range(DEPTH):
        shift, scale, gate = jnp.split(c_act @ ada_w[layer] + ada_b[layer], 3, axis=-1)
        u = modulate(rms_norm(h, norm_g[layer]), shift, scale)
        if layer < N_A_LAYERS:
            a = layer
            val, z = jnp.split(u @ a_w_in[a], 2, axis=-1)
            pooled = multiscale_pool(val).astype(val.dtype)
            mixed = jnp.einsum('bsgi,gio->bsgo', pooled, a_w_group[a]).reshape(B, S, D_INNER)
            mixed = mixed * a_scale[a]
            y = (mixed * jax.nn.silu(z)) @ a_w_out[a]
        else:
            if layer == N_A_LAYERS:
                kv_shift, kv_scale = jnp.split(c_act @ kv_ada_w + kv_ada_b, 2, axis=-1)
                hk = modulate(rms_norm(h, kv_norm_g), kv_shift, kv_scale)
                k, v = jnp.split(hk @ w_kv, 2, axis=-1)
                k = k.reshape(B, S, N_HEADS, HEAD_DIM)
                v = v.reshape(B, S, N_HEADS, HEAD_DIM)
            bi = layer - N_A_LAYERS
            qv, z = jnp.split(u @ b_w_in[bi], 2, axis=-1)
            q = qv.reshape(B, S, N_HEADS, HEAD_DIM)
            att = chunk_band_attention(q, k, v, b_rel_bias[bi])
            y = (att * jax.nn.silu(z)) @ b_w_out[bi]
        h = h + gate[:, None, :] * y
    return rms_norm(h, final_g)
```

```python
import numpy as np
from contextlib import ExitStack
import concourse.bass as bass
import concourse.mybir as mybir
from concourse.bass_utils import run_bass_kernel_spmd

F32 = mybir.dt.float32
BF16 = mybir.dt.bfloat16
AF = mybir.ActivationFunctionType
ALU = mybir.AluOpType

EPS = 1e-6
NEG = -30000.0
POOL_WINDOWS = (2, 4, 8, 16)
PW = 143


class Cfg:
    def __init__(s, D=1024, E=2048, NT=8, T=512, WB=3):
        s.D, s.E, s.NT, s.T, s.WB = D, E, NT, T, WB
        s.KD = D // 128
        s.KE = E // 128
        s.NH = E // 128
        s.GW = E // 4
        s.CPG = s.GW // 128
        s.NB = T // 128
        s.NTOK = 128 + T + NT * T
        s.ZW = min(512, E)
        s.OW = min(256, D)
        r = 0
        s.r_adab = r; r += 4 * 3 * s.KD
        s.r_kvb = r; r += 2 * s.KD
        s.r_ng = r; r += 4 * s.KD
        s.r_kvg = r; r += s.KD
        s.r_fg = r; r += s.KD
        s.r_asc = r; r += 2 * s.KE
        s.r_c = r; r += s.KD
        s.NV = r
        s.NM = 4 * 3 * s.KD + 2 * s.KD


class Op:
    __slots__ = ("eng", "fn", "idx", "deps", "dma_slot", "dma_val", "seq",
                 "sig", "sigval", "waits", "vc")


class Prog:
    ENG = ("pe", "act", "dve", "pool", "sp")

    def __init__(self):
        self.ops = []
        self.lastw = {}
        self.readers = {}
        self.dma_cnt = {}
        self.group_slots = set()

    def add(self, eng, fn, reads=(), writes=(), dma_slot=None):
        o = Op()
        o.eng, o.fn, o.idx, o.dma_slot = eng, fn, len(self.ops), dma_slot
        o.sig = False
        o.dma_val = 0
        deps = {}
        for k in reads:
            w = self.lastw.get(k)
            if w is not None:
                deps[w.idx] = w
        for k in writes:
            w = self.lastw.get(k)
            if w is not None:
                deps[w.idx] = w
            for r in self.readers.get(k, ()):
                deps[r.idx] = r
        for k in reads:
            self.readers.setdefault(k, []).append(o)
        for k in writes:
            self.lastw[k] = o
            self.readers[k] = []
        o.deps = list(deps.values())
        if dma_slot is not None:
            c = self.dma_cnt.get(dma_slot, 0) + 1
            self.dma_cnt[dma_slot] = c
            o.dma_val = 16 * c
        self.ops.append(o)
        return o

    def finalize(self):
        seq = {e: 0 for e in self.ENG}
        for o in self.ops:
            if o.dma_slot is None:
                seq[o.eng] += 1
                o.seq = seq[o.eng]
        K = {e: {} for e in self.ENG}
        for o in self.ops:
            Ke = K[o.eng]
            cand = []
            for d in o.deps:
                if d.dma_slot is not None:
                    key = ("dma", d.dma_slot)
                    val = d.dma_val
                    if d.dma_slot in self.group_slots:
                        val = 16 * self.dma_cnt[d.dma_slot]
                else:
                    if d.eng == "pe" and o.eng == "pe" and o.dma_slot is None:
                        continue
                    key = ("eng", d.eng)
                    val = d.seq
                cand.append((d.idx, key, val, d))
            cand.sort(key=lambda t: -t[0])
            waits = []
            for _, key, val, d in cand:
                if Ke.get(key, 0) >= val:
                    continue
                if key[0] == "eng":
                    d.sig = True
                waits.append((key, val, d))
                for k2, v2 in d.vc.items():
                    if Ke.get(k2, 0) < v2:
                        Ke[k2] = v2
                if Ke.get(key, 0) < val:
                    Ke[key] = val
            o.waits = waits
            vc = dict(Ke)
            if o.dma_slot is None:
                vc[("eng", o.eng)] = o.seq
            else:
                key = ("dma", o.dma_slot)
                val = o.dma_val
                if o.dma_slot in self.group_slots:
                    val = 16 * self.dma_cnt[o.dma_slot]
                if vc.get(key, 0) < val:
                    vc[key] = val
            o.vc = vc
        cnt = {e: 0 for e in self.ENG}
        for o in self.ops:
            if o.dma_slot is None and o.sig:
                cnt[o.eng] += 1
                o.sigval = cnt[o.eng]
        self.sig_counts = cnt

    def emit(self, nc, es):
        self.finalize()
        sems = {}
        for e in ("pe", "act", "dve", "pool"):
            sems[("eng", e)] = es.enter_context(nc.semaphore("s_" + e))
        for slot in self.dma_cnt:
            sems[("dma", slot)] = es.enter_context(nc.semaphore("d_" + str(slot)))
        block = es.enter_context(nc.Block())
        per = {e: [o for o in self.ops if o.eng == e] for e in self.ENG}

        def run(eng_name, eng):
            for o in per[eng_name]:
                ws_ = [(sems[key], d.sigval if key[0] == "eng" else val)
                       for key, val, d in o.waits]
                emb = None
                if ws_ and o.dma_slot is None:
                    emb = ws_.pop()
                for sm, v in ws_:
                    eng.wait_ge(sm, v)
                inst = o.fn(eng)
                if emb is not None:
                    inst._wait_ge(emb[0], emb[1])
                if o.dma_slot is not None:
                    inst.then_inc(sems[("dma", o.dma_slot)], 16)
                elif o.sig:
                    inst.then_inc(sems[("eng", o.eng)], 1)

        @block.tensor
        def _(e):
            run("pe", e)

        @block.scalar
        def _(e):
            run("act", e)

        @block.vector
        def _(e):
            run("dve", e)

        @block.gpsimd
        def _(e):
            run("pool", e)

        @block.sync
        def _(e):
            run("sp", e)


def weight_schedule(cfg):
    KD, KE = cfg.KD, cfg.KE
    S = []

    def ada(l):
        cols = 3 * cfg.D
        pw = 512 if cols % 512 == 0 else 256
        for c in range(cols // pw):
            S.append(("ada", l, c, pw))

    def kvada():
        cols = 2 * cfg.D
        pw = 512 if cols % 512 == 0 else 256
        for c in range(cols // pw):
            S.append(("kvada", 0, c, pw))

    for l in range(4):
        ada(l)
    kvada()

    def alayer(a, only_val):
        if not only_val:
            for c in range(cfg.E // cfg.ZW):
                S.append(("a_z", a, c, cfg.ZW))
        for g in range(4):
            S.append(("a_val", a, g, cfg.GW))
            if not only_val:
                S.append(("a_wg", a, g, cfg.GW))
        if not only_val:
            for c in range(cfg.D // cfg.OW):
                S.append(("a_wo", a, c, cfg.OW))

    def kv():
        for c in range(cfg.E // cfg.ZW):
            S.append(("kv_k", 0, c, cfg.ZW))
        for c in range(cfg.E // cfg.ZW):
            S.append(("kv_v", 0, c, cfg.ZW))

    def blayer(b):
        npc = cfg.E // cfg.ZW
        for c in range(npc):
            S.append(("b_z", b, c, cfg.ZW))
        for c in range(npc):
            S.append(("b_q", b, c, cfg.ZW))
        for c in range(cfg.D // cfg.OW):
            S.append(("b_wo", b, c, cfg.OW))

    alayer(0, False)
    alayer(1, True)
    alayer(0, False)
    alayer(1, False)
    kv()
    for _ in range(cfg.NT):
        alayer(0, False)
        alayer(1, False)
        kv()
        blayer(0)
        blayer(1)
    return S


def build_program(cfg):
    D, E, KD, KE, NH, GW, CPG, T, NT = (cfg.D, cfg.E, cfg.KD, cfg.KE, cfg.NH,
                                        cfg.GW, cfg.CPG, cfg.T, cfg.NT)
    nc = bass.Bass("TRN2", target_bir_lowering=False)
    es = ExitStack()

    def dram(name, shape, kind="ExternalInput"):
        return nc.dram_tensor(name, list(shape), F32, kind=kind).ap()

    xin = dram("xin", [cfg.NTOK, D])
    vecs = dram("vecs", [cfg.NV, 128])
    ada_w = dram("ada_w", [4, D, 3 * D])
    kv_ada_w = dram("kv_ada_w", [D, 2 * D])
    a_w_in = dram("a_w_in", [2, D, 2 * E])
    a_w_group = dram("a_w_group", [2, 4, GW, GW])
    a_w_out = dram("a_w_out", [2, E, D])
    w_kv = dram("w_kv", [D, 2 * E])
    b_w_in = dram("b_w_in", [2, D, 2 * E])
    b_w_out = dram("b_w_out", [2, E, D])
    biasn = dram("biasn", [2, NH, 128, 256])
    cfar = dram("cfar", [128, 2 * NH])
    hmask = dram("hmask", [128, 1])
    pm_std = dram("pm_std", [128, 4, PW])
    pm0 = dram("pm0", [128, 4, PW + 15])
    identm = dram("identm", [128, 128])
    out = dram("out", [NT * T, D], kind="ExternalOutput")
    escr = nc.dram_tensor("escr", [2, 128, NH * 256], BF16, kind="Internal").ap()

    def sb(name, shape, dt):
        return es.enter_context(nc.sbuf_tensor(name, list(shape), dt))

    def psb(name):
        return es.enter_context(nc.psum_tensor(name, [128, 512], F32))

    h = sb("h", [128, KD, T], F32)
    u = sb("u", [128, KD, T], BF16)
    kring = sb("kring", [128, NH, 2 * T], BF16)
    vring = sb("vring", [128, 2 * cfg.NB, E], BF16)
    hist = sb("hist", [128, 2, E], BF16)
    sz = sb("sz", [128, KE, T], BF16)
    bufq = sb("bufq", [128, KE * T], BF16)
    wbuf = [sb(f"wbuf{i}", [128, 4096], BF16) for i in range(cfg.WB)]
    enear = sb("enear", [128, NH, 256], BF16)
    bst = [sb(f"bst{i}", [128, 256], F32) for i in range(2)]
    est = [sb(f"est{i}", [128, 256], BF16) for i in range(2)]
    xst = [sb(f"xst{i}", [128, D], F32) for i in range(2)]
    ost = [sb(f"ost{i}", [128, 512], F32) for i in range(4)]
    NSQ = 4
    sq = [sb(f"sq{i}", [128, T], BF16) for i in range(NSQ)]
    rt = sb("rt", [128, T], F32)
    rstd = sb("rstd", [128, T], F32)
    tmpf = [sb(f"tmpf{i}", [128, T], F32) for i in range(2)]
    rd = [sb(f"rd{i}", [128, T], F32) for i in range(2)]
    NPT = 6
    LOOK = 3
    pT = [sb(f"pT{i}", [128, 512], BF16) for i in range(NPT)]
    vst = sb("vst", [128, 128], F32)
    vecT = sb("vecT", [128, cfg.NV], F32)
    modT = sb("modT", [128, cfg.NM], F32)
    gm = sb("gm", [128, 5 * KD], F32)
    cact = sb("cact", [128, KD], BF16)
    cfar_t = sb("cfar_t", [128, 2 * NH], F32)
    cfsc = sb("cfsc", [128, 2 * NH], F32)
    cneg = sb("cneg", [128, 2 * NH], F32)
    chalo = sb("chalo", [128, 2 * NH], F32)
    hm_t = sb("hm_t", [128, 1], F32)
    pms = sb("pms", [128, 4, PW], BF16)
    pm0s = sb("pm0s", [128, 4, PW + 15], BF16)
    ident = sb("ident", [128, 128], F32)
    ones = sb("ones", [128, 128], BF16)
    epsb = sb("epsb", [128, 1], F32)
    fcell = sb("fcell", [128, 2], F32)

    banks = [psb(f"ps{i}") for i in range(8)]

    P = Prog()
    P.group_slots.add("const_sp")
    P.group_slots.add("const_pool")

    rr = {"a": 0, "s": 0, "evac": 0, "pt": 0, "xst": 0, "ost": 0, "bst": 0, "sq": 0,
          "tmpf": 0}

    def nextbank():
        b = rr["a"] % 4
        rr["a"] += 1
        return b

    def mm(bank, out_ap, lhsT, rhs, start, reads, skip=False):
        def f(e, out_ap=out_ap, lhsT=lhsT, rhs=rhs, start=start, skip=skip):
            return e.matmul(out_ap, lhsT, rhs, start=start, stop=True,
                            skip_group_check=True)
        return P.add("pe", f, reads=reads, writes=[("ps", bank)])

    def act(out_ap, in_ap, func, reads, writes, bias=None, scale=None):
        def f(e):
            kw = {}
            if bias is not None:
                kw["bias"] = bias
            if scale is not None:
                kw["scale"] = scale
            return e.activation(out=out_ap, in_=in_ap, func=func, **kw)
        return P.add("act", f, reads=reads, writes=writes)

    def dve(fn, reads, writes):
        return P.add("dve", fn, reads=reads, writes=writes)

    def evac(out_ap, in_ap, reads, writes, prefer=None):
        rr["evac"] += 1
        if prefer == "act" or (prefer is None and rr["evac"] % 2 == 0):
            return act(out_ap, in_ap, AF.Copy, reads, writes)
        return dve(lambda e: e.tensor_copy(out=out_ap, in_=in_ap), reads, writes)

    def dma(queue, out_ap, in_ap, slot, reads, writes):
        return P.add(queue, lambda e: e.dma_start(out=out_ap, in_=in_ap),
                     reads=reads, writes=writes, dma_slot=slot)

    sched = weight_schedule(cfg)
    ws = {"next_load": 0, "next_use": 0}

    def piece_src(desc):
        kind, l, c, pw = desc
        if kind == "ada":
            src, kc = ada_w[l, :, c * pw:(c + 1) * pw], KD
        elif kind == "kvada":
            src, kc = kv_ada_w[:, c * pw:(c + 1) * pw], KD
        elif kind == "a_z":
            src, kc = a_w_in[l, :, E + c * pw:E + (c + 1) * pw], KD
        elif kind == "a_val":
            src, kc = a_w_in[l, :, c * pw:(c + 1) * pw], KD
        elif kind == "a_wg":
            src, kc = a_w_group[l, c, :, :], CPG
        elif kind == "a_wo":
            src, kc = a_w_out[l, :, c * pw:(c + 1) * pw], KE
        elif kind == "kv_k":
            src, kc = w_kv[:, c * pw:(c + 1) * pw], KD
        elif kind == "kv_v":
            src, kc = w_kv[:, E + c * pw:E + (c + 1) * pw], KD
        elif kind == "b_q":
            src, kc = b_w_in[l, :, c * pw:(c + 1) * pw], KD
        elif kind == "b_z":
            src, kc = b_w_in[l, :, E + c * pw:E + (c + 1) * pw], KD
        elif kind == "b_wo":
            src, kc = b_w_out[l, :, c * pw:(c + 1) * pw], KE
        else:
            raise ValueError(kind)
        return src.rearrange("(kc p) n -> p kc n", p=128), kc, pw

    def issue_load():
        i = ws["next_load"]
        if i >= len(sched):
            return
        ws["next_load"] += 1
        slot = i % cfg.WB
        src, kc, pw = piece_src(sched[i])
        dst = wbuf[slot][:, 0:kc * pw].rearrange("p (kc n) -> p kc n", kc=kc)
        dma("pool", dst, src, f"w{slot}", reads=[], writes=[("wbuf", slot)])

    def get_piece(kind, l, c):
        i = ws["next_use"]
        assert sched[i][:3] == (kind, l, c), (sched[i], kind, l, c)
        ws["next_use"] += 1
        slot = i % cfg.WB
        _, kc, pw = piece_src(sched[i])
        view = wbuf[slot][:, 0:kc * pw].rearrange("p (kc n) -> p kc n", kc=kc)
        return view, ("wbuf", slot)

    def release_piece():
        issue_load()

    P.add("dve", lambda e: e.memset(ones[:], 1.0), writes=[("ones",)])
    P.add("dve", lambda e: e.memset(epsb[:], EPS), writes=[("epsb",)])
    P.add("dve", lambda e: e.memset(hist[:].rearrange("p a e -> p (a e)"), 0.0),
          writes=[("hist", 0), ("hist", 1)])
    dma("sp", ident[:], identm[:, :], "const_sp", [], [("ident",)])
    dma("sp", cfar_t[:], cfar[:, :], "const_sp", [], [("cfar",)])
    dma("sp", hm_t[:], hmask[:, :], "const_sp", [], [("hm",)])
    dma("pool", pms[:], pm_std[:, :, :], "const_pool", [], [("pms",)])
    dma("pool", pm0s[:], pm0[:, :, :], "const_pool", [], [("pm0s",)])
    for i in range(cfg.WB):
        issue_load()

    r0 = 0
    while r0 < cfg.NV:
        n = min(128, cfg.NV - r0)
        dma("sp", vst[0:n, :], vecs[r0:r0 + n, :], "vst", [], [("vst",)])
        b = nextbank()
        P.add("pe", lambda e, n=n, b=b: e.transpose(
            banks[b][:, 0:n], vst[0:n, :], ident[0:n, 0:n]),
            reads=[("vst",), ("ident",)], writes=[("ps", b)])
        dve(lambda e, n=n, b=b, r0=r0: e.tensor_copy(
            out=vecT[:, r0:r0 + n], in_=banks[b][:, 0:n]),
            reads=[("ps", b)], writes=[("vecT",)])
        r0 += n

    act(cact[:], vecT[:, cfg.r_c:cfg.r_c + KD], AF.Silu, [("vecT",)], [("cact",)])
    dve(lambda e: e.tensor_copy(out=cfsc[:], in_=cfar_t[:]), [("cfar",)], [("cfsc",)])
    dve(lambda e: e.tensor_scalar(out=cneg[:], in0=cfar_t[:], scalar1=-1.0,
                                  scalar2=None, op0=ALU.mult),
        [("cfar",)], [("cneg",)])
    dve(lambda e: e.tensor_scalar(out=chalo[:], in0=cfar_t[:], scalar1=hm_t[:, 0:1],
                                  scalar2=None, op0=ALU.add),
        [("cfar",), ("hm",)], [("chalo",)])

    for bi_ in range(2):
        for hh in range(NH):
            s_ = rr["bst"] % 2
            rr["bst"] += 1
            dma("sp", bst[s_][:], biasn[bi_, hh, :, :], f"b{s_}", [], [("bst", s_)])
            act(est[s_][:], bst[s_][:], AF.Exp, [("bst", s_), ("cneg",)], [("est", s_)],
                bias=cneg[:, bi_ * NH + hh:bi_ * NH + hh + 1], scale=1.0)
            dma("sp", escr[bi_, :, hh * 256:(hh + 1) * 256], est[s_][:], f"es{s_}",
                [("est", s_)], [("escr", bi_, s_)])

    def load_enear(bi_):
        dma("sp", enear[:].rearrange("p h c -> p (h c)"), escr[bi_, :, :], "en",
            [("escr", bi_, 0), ("escr", bi_, 1)], [("enear", hh) for hh in range(NH)])

    mb = 7
    first = True
    for desc in [d for d in sched if d[0] in ("ada", "kvada")]:
        kind, l, c, pw = desc
        wv, wkey = get_piece(kind, l, c)
        base = (l * 3 * KD if kind == "ada" else 4 * 3 * KD) + c * (pw // 128)
        for j in range(pw // 128):
            col = base + j
            for kc in range(KD):
                mm(mb, banks[mb][:, col:col + 1], wv[:, kc, j * 128:(j + 1) * 128],
                   cact[:, kc:kc + 1], start=(kc == 0),
                   reads=[wkey, ("cact",)], skip=True)
        release_piece()
    dve(lambda e: e.tensor_tensor(out=modT[:], in0=banks[mb][:, 0:cfg.NM],
                                  in1=vecT[:, cfg.r_adab:cfg.r_adab + cfg.NM],
                                  op=ALU.add),
        [("ps", mb), ("vecT",)], [("modT",)])
    for l in range(4):
        sc0 = l * 3 * KD + KD
        dve(lambda e, l=l, sc0=sc0: e.scalar_tensor_tensor(
            out=gm[:, l * KD:(l + 1) * KD], in0=modT[:, sc0:sc0 + KD], scalar=1.0,
            in1=vecT[:, cfg.r_ng + l * KD:cfg.r_ng + (l + 1) * KD],
            op0=ALU.add, op1=ALU.mult), [("modT",), ("vecT",)], [("gm",)])
    kvb = 4 * 3 * KD
    dve(lambda e: e.scalar_tensor_tensor(
        out=gm[:, 4 * KD:5 * KD], in0=modT[:, kvb + KD:kvb + 2 * KD], scalar=1.0,
        in1=vecT[:, cfg.r_kvg:cfg.r_kvg + KD], op0=ALU.add, op1=ALU.mult),
        [("modT",), ("vecT",)], [("gm",)])

    def shift_col(l, k):
        return modT[:, l * 3 * KD + k:l * 3 * KD + k + 1]

    def gate_col(l, k):
        return modT[:, l * 3 * KD + 2 * KD + k:l * 3 * KD + 2 * KD + k + 1]

    SB_ = 7
    st = {"ready": False, "nmm": 0}
    xq = {"issued": {}}

    def issue_x(tok0, tb):
        if (tok0, tb) in xq["issued"]:
            return
        s = rr["xst"] % 2
        rr["xst"] += 1
        dma("sp", xst[s][:], xin[tok0 + tb * 128:tok0 + (tb + 1) * 128, :],
            f"x{s}", [], [("xst", s)])
        xq["issued"][(tok0, tb)] = s

    def load_x(tok0, nb, Tt, nxt=None):
        for tb in range(nb):
            issue_x(tok0, tb)
            if tb + 1 < nb:
                issue_x(tok0, tb + 1)
            s = xq["issued"][(tok0, tb)]
            for k0 in range(0, KD, 4):
                nk = min(4, KD - k0)
                b = nextbank()
                for kk in range(nk):
                    k = k0 + kk
                    P.add("pe", lambda e, b=b, kk=kk, k=k, s=s: e.transpose(
                        banks[b][:, kk * 128:(kk + 1) * 128],
                        xst[s][:, k * 128:(k + 1) * 128], ident[:]),
                        reads=[("xst", s), ("ident",)], writes=[("ps", b)])
                evac(h[:, k0:k0 + nk, tb * 128:(tb + 1) * 128],
                     banks[b][:, 0:nk * 128].rearrange("p (k t) -> p k t", k=nk),
                     [("ps", b)], [("h", k) for k in range(k0, k0 + nk)])
                s_ = rr["sq"] % NSQ
                rr["sq"] += 1
                act(sq[s_][:, 0:nk * 128].rearrange("p (k t) -> p k t", k=nk),
                    h[:, k0:k0 + nk, tb * 128:(tb + 1) * 128], AF.Square,
                    [("h", k) for k in range(k0, k0 + nk)], [("sq", s_)])
                for kk in range(nk):
                    mm(SB_, banks[SB_][:, tb * 128:(tb + 1) * 128], ones[:],
                       sq[s_][:, kk * 128:(kk + 1) * 128], start=(st["nmm"] == 0),
                       reads=[("ones",), ("sq", s_)])
                    st["nmm"] += 1
        st["nmm"] = KD
        stats_finish(Tt)
        if nxt is not None:
            issue_x(nxt[0], 0)
            if nxt[1] > 1:
                issue_x(nxt[0], 1)


    def stats_square(k, Tt):
        s_ = rr["sq"] % NSQ
        rr["sq"] += 1
        act(sq[s_][:, 0:Tt], h[:, k, 0:Tt], AF.Square, [("h", k)], [("sq", s_)])
        return s_

    def stats_mm(s_, Tt):
        mm(SB_, banks[SB_][:, 0:Tt], ones[:], sq[s_][:, 0:Tt], start=(st["nmm"] == 0),
           reads=[("ones",), ("sq", s_)])
        st["nmm"] += 1

    def stats_finish(Tt):
        assert st["nmm"] == KD
        st["nmm"] = 0
        act(rt[:, 0:Tt], banks[SB_][:, 0:Tt], AF.Ln, [("ps", SB_), ("epsb",)], [("rt",)],
            bias=epsb[:, 0:1], scale=1.0 / D)
        act(rstd[:, 0:Tt], rt[:, 0:Tt], AF.Exp, [("rt",)], [("rstd",)], scale=-0.5)
        st["ready"] = True

    def rms_stats(Tt):
        if st["ready"]:
            st["ready"] = False
            return
        pend = []
        for k in range(KD):
            pend.append(stats_square(k, Tt))
            if len(pend) > 1:
                stats_mm(pend.pop(0), Tt)
        for s_ in pend:
            stats_mm(s_, Tt)
        stats_finish(Tt)
        st["ready"] = False

    def modulate(Tt, gcol0, shift_fn):
        for k in range(KD):
            s = rr["tmpf"] % 2
            rr["tmpf"] += 1
            dve(lambda e, k=k, s=s: e.tensor_tensor(
                out=tmpf[s][:, 0:Tt], in0=h[:, k, 0:Tt], in1=rstd[:, 0:Tt], op=ALU.mult),
                [("h", k), ("rstd",)], [("tmpf", s)])
            act(u[:, k, 0:Tt], tmpf[s][:, 0:Tt], AF.Identity,
                [("tmpf", s), ("gm",), ("modT",)], [("u", k)],
                bias=shift_fn(k), scale=gm[:, gcol0 + k:gcol0 + k + 1])

    def proj_fm(kind, l, width, Tt, consume):
        npieces = E // width
        for c in range(npieces):
            wv, wkey = get_piece(kind, l, c)
            nj = width // 128
            if c == 0 and nj <= 4:
                bs = [nextbank() for _ in range(nj)]
                for kc in range(KD):
                    for j in range(nj):
                        mm(bs[j], banks[bs[j]][:, 0:Tt], wv[:, kc, j * 128:(j + 1) * 128],
                           u[:, kc, 0:Tt], start=(kc == 0), reads=[wkey, ("u", kc)])
                for j in range(nj):
                    consume(c * nj + j, bs[j])
            else:
                for j in range(nj):
                    oc = c * nj + j
                    b = nextbank()
                    for kc in range(KD):
                        mm(b, banks[b][:, 0:Tt], wv[:, kc, j * 128:(j + 1) * 128],
                           u[:, kc, 0:Tt], start=(kc == 0),
                           reads=[wkey, ("u", kc)])
                    consume(oc, b)
            release_piece()

    def out_proj(kind, l, layer, Tt):
        pend = []
        for c in range(D // cfg.OW):
            wv, wkey = get_piece(kind, l, c)
            nj = cfg.OW // 128
            bs = [nextbank() for _ in range(nj)]
            if c == 0:
                for ec in range(KE):
                    for j in range(nj):
                        mm(bs[j], banks[bs[j]][:, 0:Tt], wv[:, ec, j * 128:(j + 1) * 128],
                           sz[:, ec, 0:Tt], start=(ec == 0), reads=[wkey, ("sz", ec)])
            for j in range(nj):
                dc = c * nj + j
                b = bs[j]
                if c != 0:
                    for ec in range(KE):
                        mm(b, banks[b][:, 0:Tt], wv[:, ec, j * 128:(j + 1) * 128],
                           sz[:, ec, 0:Tt], start=(ec == 0),
                           reads=[wkey, ("sz", ec)])
                dve(lambda e, b=b, dc=dc: e.scalar_tensor_tensor(
                    out=h[:, dc, 0:Tt], in0=banks[b][:, 0:Tt], scalar=gate_col(layer, dc),
                    in1=h[:, dc, 0:Tt], op0=ALU.mult, op1=ALU.add),
                    [("ps", b), ("modT",), ("h", dc)], [("h", dc)])
                pend.append(stats_square(dc, Tt))
                if len(pend) > 2:
                    stats_mm(pend.pop(0), Tt)
            release_piece()
        for s_ in pend:
            stats_mm(s_, Tt)
        stats_finish(Tt)

    val_v = [bufq[:, i * cfg.NB * GW:(i + 1) * cfg.NB * GW].rearrange(
        "p (b n) -> p b n", b=cfg.NB) for i in range(2)]
    off = 2 * cfg.NB * GW
    pooled_v = [bufq[:, off + i * CPG * T: off + (i + 1) * CPG * T].rearrange(
        "p (c t) -> p c t", c=CPG) for i in range(2)]
    q_v = bufq[:, 0:KE * T].rearrange("p (h t) -> p h t", h=KE)

    BQ = ("bufq_mode",)

    def fence():
        dve(lambda e: e.memset(fcell[:, 0:1], 0.0), [], [BQ])

    def a_layer(a, Tt, only_val, first_real):
        layer = a
        nb = Tt // 128
        if a == 0:
            fence()
        rms_stats(Tt)
        modulate(Tt, layer * KD, lambda k: shift_col(layer, k))
        if not only_val:
            def z_consume(oc, b):
                act(sz[:, oc, 0:Tt], banks[b][:, 0:Tt], AF.Silu, [("ps", b)], [("sz", oc)])
            proj_fm("a_z", a, cfg.ZW, Tt, z_consume)
        for g in range(4):
            vs = g % 2
            wv, wkey = get_piece("a_val", a, g)
            for tb in range(nb):
                b = nextbank()
                for kc in range(KD):
                    mm(b, banks[b][:, 0:GW], u[:, kc, tb * 128:(tb + 1) * 128],
                       wv[:, kc, 0:GW], start=(kc == 0), reads=[wkey, ("u", kc)])
                evac(val_v[vs][:, tb, 0:GW], banks[b][:, 0:GW], [("ps", b), BQ],
                     [("val", vs, tb)])
            release_piece()
            if not only_val:
                wg, wgkey = get_piece("a_wg", a, g)
                for ci in range(CPG):
                    ec = g * CPG + ci
                    b = nextbank()
                    if first_real:
                        ph = pm0s[:, g, PW:PW + 15]
                    else:
                        ph = pms[:, g, 128:PW]
                    mm(b, banks[b][:, 0:15], hist[:, a, ec * 128:(ec + 1) * 128], ph,
                       start=True, reads=[("hist", a), ("pms",), ("pm0s",)], skip=True)
                    for sbk in range(nb):
                        ncols = min(PW, Tt - 128 * sbk)
                        if first_real and sbk == 0:
                            pr = pm0s[:, g, 0:ncols]
                        else:
                            pr = pms[:, g, 0:ncols]
                        mm(b, banks[b][:, 128 * sbk:128 * sbk + ncols],
                           val_v[vs][:, sbk, ci * 128:(ci + 1) * 128], pr, start=False,
                           reads=[("val", vs, sbk), ("pms",), ("pm0s",), BQ], skip=True)
                    evac(pooled_v[vs][:, ci, 0:Tt], banks[b][:, 0:Tt], [("ps", b), BQ],
                         [("pooled", vs, ci)])
            dve(lambda e, vs=vs, g=g: e.tensor_copy(
                out=hist[:, a, g * GW:(g + 1) * GW], in_=val_v[vs][:, nb - 1, 0:GW]),
                [("val", vs, nb - 1), BQ], [("hist", a)])
            if not only_val:
                for co in range(CPG):
                    oc = g * CPG + co
                    b = nextbank()
                    for ci in range(CPG):
                        mm(b, banks[b][:, 0:Tt], wg[:, ci, co * 128:(co + 1) * 128],
                           pooled_v[vs][:, ci, 0:Tt], start=(ci == 0),
                           reads=[wgkey, ("pooled", vs, ci), BQ])
                    asc = vecT[:, cfg.r_asc + a * KE + oc:cfg.r_asc + a * KE + oc + 1]
                    dve(lambda e, b=b, oc=oc, asc=asc: e.scalar_tensor_tensor(
                        out=sz[:, oc, 0:Tt], in0=banks[b][:, 0:Tt], scalar=asc,
                        in1=sz[:, oc, 0:Tt], op0=ALU.mult, op1=ALU.mult),
                        [("ps", b), ("vecT",), ("sz", oc)], [("sz", oc)])
                release_piece()
        if not only_val:
            out_proj("a_wo", a, layer, Tt)

    def kv_layer(slot):
        Tt = T
        rms_stats(Tt)
        kvb_ = 4 * 3 * KD
        modulate(Tt, 4 * KD, lambda k: modT[:, kvb_ + k:kvb_ + k + 1])

        def k_consume(oc, b):
            evac(kring[:, oc, slot * T:(slot + 1) * T], banks[b][:, 0:Tt], [("ps", b)],
                 [("kring", slot, oc)])
        proj_fm("kv_k", 0, cfg.ZW, Tt, k_consume)
        for c in range(E // cfg.ZW):
            wv, wkey = get_piece("kv_v", 0, c)
            for tb in range(cfg.NB):
                b = nextbank()
                for kc in range(KD):
                    mm(b, banks[b][:, 0:cfg.ZW], u[:, kc, tb * 128:(tb + 1) * 128],
                       wv[:, kc, 0:cfg.ZW], start=(kc == 0), reads=[wkey, ("u", kc)])
                evac(vring[:, slot * cfg.NB + tb, c * cfg.ZW:(c + 1) * cfg.ZW],
                     banks[b][:, 0:cfg.ZW], [("ps", b)], [("vring", slot * cfg.NB + tb, c)])
            release_piece()

    sm_scale = float(128 ** -0.5)

    def b_layer(bi, slot, first_real, reuse_stats):
        layer = 2 + bi
        Tt = T
        if bi == 0:
            fence()
        if not reuse_stats:
            rms_stats(Tt)
        modulate(Tt, layer * KD, lambda k: shift_col(layer, k))

        if first_real:
            groups = [[3], [4], [0, 2], [1], [6], [7, 5]]
        else:
            groups = [[3], [4], [0, 2], [1, 6], [7, 5]]
        NG = len(groups)
        steps = [(hh, gi) for hh in range(NH) for gi in range(NG)]
        info = {}

        def blk(j):
            a_lo = max(0, 2 * j - 8)
            a_hi = min(7, 2 * j + 1)
            return a_lo, a_hi, 64 * a_lo, 64 * (a_hi + 1)

        def stage1(idx):
            hh, gi = steps[idx]
            lh = bi * NH + hh
            sbk = rr["s"] % 3
            rr["s"] += 1
            ps_ = rr["pt"] % NPT
            rr["pt"] += 1
            off = 0
            members = []
            for n_, j in enumerate(groups[gi]):
                a_lo, a_hi, qlo, qhi = blk(j)
                ncols = qhi - qlo
                if j < 4:
                    kcol = (1 - slot) * T + j * 128
                    vblk = (1 - slot) * cfg.NB + j
                    kkey = ("kring", 1 - slot, hh)
                else:
                    kcol = slot * T + (j - 4) * 128
                    vblk = slot * cfg.NB + (j - 4)
                    kkey = ("kring", slot, hh)
                mm(sbk, banks[sbk][:, off:off + ncols], kring[:, hh, kcol:kcol + 128],
                   q_v[:, hh, qlo:qhi], start=(n_ == 0), reads=[kkey, ("q", hh), BQ])
                members.append((j, off, ncols, qlo, qhi, vblk, a_lo, a_hi))
                off += ncols
            total = off
            assert total <= 512
            if first_real and groups[gi][0] < 4:
                bcol = chalo[:, lh:lh + 1]
            else:
                bcol = cfsc[:, lh:lh + 1]
            act(pT[ps_][:, 0:total], banks[sbk][:, 0:total], AF.Exp,
                [("ps", sbk), ("cfsc",), ("chalo",)], [("pT", ps_)],
                bias=bcol, scale=sm_scale)
            for (j, off, ncols, qlo, qhi, vblk, a_lo, a_hi) in members:
                n0 = max(a_lo, 2 * j - 8)
                n1 = min(a_hi, 2 * j - 5)
                if n1 >= n0:
                    c0 = off + (n0 - a_lo) * 64
                    c1 = off + (n1 + 1 - a_lo) * 64
                    e0 = (n0 - (2 * j - 8)) * 64
                    e1 = (n1 + 1 - (2 * j - 8)) * 64
                    dve(lambda e, ps_=ps_, c0=c0, c1=c1, e0=e0, e1=e1, hh=hh: e.tensor_tensor(
                        out=pT[ps_][:, c0:c1], in0=pT[ps_][:, c0:c1],
                        in1=enear[:, hh, e0:e1], op=ALU.mult),
                        [("pT", ps_), ("enear", hh)], [("pT", ps_)])
                if 2 * j + 1 <= 7:
                    m0 = off + (2 * j + 1 - a_lo) * 64
                    dve(lambda e, ps_=ps_, m0=m0: e.memset(pT[ps_][0:64, m0:m0 + 64], 0.0),
                        [], [("pT", ps_)])
            info[idx] = (ps_, members)

        def stage2(idx):
            hh, gi = steps[idx]
            ps_, members = info.pop(idx)
            ob = 4 + (hh % 2)
            db = 6 + (hh % 2)
            for n_, (j, off, ncols, qlo, qhi, vblk, a_lo, a_hi) in enumerate(members):
                first = (gi == 0 and n_ == 0)
                vkeys = [("vring", vblk, c) for c in range(E // cfg.ZW)]
                mm(ob, banks[ob][:, qlo:qhi], vring[:, vblk, hh * 128:(hh + 1) * 128],
                   pT[ps_][:, off:off + ncols], start=first, reads=vkeys + [("pT", ps_)])
            for n_, (j, off, ncols, qlo, qhi, vblk, a_lo, a_hi) in enumerate(members):
                first = (gi == 0 and n_ == 0)
                mm(db, banks[db][:, qlo:qhi], ones[:], pT[ps_][:, off:off + ncols],
                   start=first, reads=[("ones",), ("pT", ps_)])
            if gi == NG - 1:
                r = hh % 2
                act(rd[r][:, 0:Tt], banks[db][:, 0:Tt], AF.Ln, [("ps", db)], [("rd", r)])
                act(rd[r][:, 0:Tt], rd[r][:, 0:Tt], AF.Exp, [("rd", r)], [("rd", r)],
                    scale=-1.0)
                dve(lambda e, r=r, hh=hh: e.tensor_tensor(
                    out=rd[r][:, 0:Tt], in0=rd[r][:, 0:Tt], in1=sz[:, hh, 0:Tt], op=ALU.mult),
                    [("rd", r), ("sz", hh)], [("rd", r)])
                dve(lambda e, r=r, hh=hh, ob=ob: e.tensor_tensor(
                    out=sz[:, hh, 0:Tt], in0=banks[ob][:, 0:Tt], in1=rd[r][:, 0:Tt],
                    op=ALU.mult), [("ps", ob), ("rd", r)], [("sz", hh)])

        HPP = cfg.ZW // 128
        npc = E // cfg.ZW
        pieces = {}
        PB = 3

        def proj_items(kind, hh, rotate=False):
            c, jj = divmod(hh, HPP)
            items = []
            bk = [PB]
            for kc in range(KD):
                def f(kc=kc):
                    if (kind, c) not in pieces:
                        pieces[(kind, c)] = get_piece(kind, bi, c)
                    wv, wkey = pieces[(kind, c)]
                    if kc == 0 and rotate:
                        bk[0] = nextbank()
                    pb = bk[0]
                    mm(pb, banks[pb][:, 0:Tt], wv[:, kc, jj * 128:(jj + 1) * 128],
                       u[:, kc, 0:Tt], start=(kc == 0), reads=[wkey, ("u", kc)])
                    if kc == KD - 1:
                        if kind == "b_q":
                            evac(q_v[:, hh, 0:Tt], banks[pb][:, 0:Tt], [("ps", pb), BQ],
                                 [("q", hh)], prefer="dve")
                        else:
                            act(sz[:, hh, 0:Tt], banks[pb][:, 0:Tt], AF.Silu,
                                [("ps", pb)], [("sz", hh)])
                        if jj == HPP - 1 or hh == NH - 1:
                            release_piece()
                items.append(f)
            return items

        nfirst = min(HPP, NH, 4)
        first_items = [proj_items("b_z", hh, rotate=True) for hh in range(nfirst)]
        for kc in range(KD):
            for hh in range(nfirst):
                first_items[hh][kc]()
        for hh in range(nfirst, NH):
            for f in proj_items("b_z", hh, rotate=True):
                f()
        for hh in range(min(HPP, NH)):
            for f in proj_items("b_q", hh, rotate=True):
                f()
        fifo = []
        for hh in range(HPP, NH):
            fifo.extend(proj_items("b_q", hh))
        fpos = [0]

        def do_work(idx):
            per_step = max(1, KD // 4)
            for _ in range(per_step):
                if fpos[0] < len(fifo):
                    fifo[fpos[0]]()
                    fpos[0] += 1

        def sbank():
            b = rr["s"] % 3
            rr["s"] += 1
            return b

        for idx in range(len(steps) + LOOK):
            if idx < len(steps):
                do_work(idx)
                stage1(idx)
            if idx >= LOOK:
                stage2(idx - LOOK)
        while fpos[0] < len(fifo):
            fifo[fpos[0]]()
            fpos[0] += 1
        if bi == 0:
            load_enear(1)
        out_proj("b_wo", bi, layer, Tt)

    NOST = 4

    def final_out(row0):
        Tt = T
        rms_stats(Tt)
        for k0 in range(0, KD, 4):
            nk = min(4, KD - k0)
            bks = [nextbank() for _ in range(cfg.NB)]
            for kk in range(nk):
                k = k0 + kk
                s_ = rr["tmpf"] % 2
                rr["tmpf"] += 1
                fg = vecT[:, cfg.r_fg + k:cfg.r_fg + k + 1]
                dve(lambda e, k=k, fg=fg, s_=s_: e.scalar_tensor_tensor(
                    out=tmpf[s_][:, 0:Tt], in0=h[:, k, 0:Tt], scalar=fg, in1=rstd[:, 0:Tt],
                    op0=ALU.mult, op1=ALU.mult),
                    [("h", k), ("vecT",), ("rstd",)], [("tmpf", s_)])
                for tb in range(cfg.NB):
                    b = bks[tb]
                    P.add("pe", lambda e, b=b, kk=kk, s_=s_, tb=tb: e.transpose(
                        banks[b][:, kk * 128:(kk + 1) * 128],
                        tmpf[s_][:, tb * 128:(tb + 1) * 128], ident[:]),
                        reads=[("tmpf", s_), ("ident",)], writes=[("ps", b)])
            for tb in range(cfg.NB):
                b = bks[tb]
                so = rr["ost"] % NOST
                rr["ost"] += 1
                evac(ost[so][:, 0:nk * 128], banks[b][:, 0:nk * 128], [("ps", b)],
                     [("ost", so)])
                dma("sp", out[row0 + tb * 128:row0 + (tb + 1) * 128, k0 * 128:(k0 + nk) * 128],
                    ost[so][:, 0:nk * 128], f"o{so}", [("ost", so)], [("outrow", so)])

    load_x(0, 1, 128, nxt=(128, cfg.NB))
    a_layer(0, 128, False, False)
    a_layer(1, 128, True, False)
    load_x(128, cfg.NB, T, nxt=(128 + T, cfg.NB))
    a_layer(0, T, False, False)
    a_layer(1, T, False, False)
    kv_layer(0)
    for it in range(NT):
        slot = (it + 1) % 2
        nxt = (128 + T + (it + 1) * T, cfg.NB) if it + 1 < NT else None
        load_x(128 + T + it * T, cfg.NB, T, nxt=nxt)
        load_enear(0)
        a_layer(0, T, False, it == 0)
        a_layer(1, T, False, it == 0)
        kv_layer(slot)
        b_layer(0, slot, it == 0, True)
        b_layer(1, slot, it == 0, False)
        final_out(it * T)
    assert ws["next_use"] == len(sched), (ws["next_use"], len(sched))

    last = P.add("sp", lambda e: e.nop(), reads=[("outrow", i) for i in range(4)])
    P.emit(nc, es)
    es.close()
    return nc


def pool_matrices():
    pm = np.zeros((128, 4, PW), np.float32)
    pstart = np.zeros((128, 4, PW), np.float32)
    for g, w in enumerate(POOL_WINDOWS):
        for t in range(PW):
            for s in range(max(0, t - w + 1), min(t, 127) + 1):
                pm[s, g, t] += 1.0 / w
                pstart[s, g, t] += 1.0 / min(t + 1, w)
            if t < 128:
                pm[t, g, t] -= 1.0
                pstart[t, g, t] -= 1.0
    return pm, pstart


def near_bias(rel_bias, NH):
    nl = rel_bias.shape[0]
    ki = np.arange(64)[:, None]
    qi = np.arange(64)[None, :]
    outb = np.full((nl, NH, 128, 256), NEG, np.float32)
    for i in range(2):
        for ap in range(4):
            delta = ap - i
            if delta < 0:
                continue
            idx = np.clip(64 * delta + qi - ki, -128, 128) + 128
            outb[:, :, i * 64:(i + 1) * 64, ap * 64:(ap + 1) * 64] = rel_bias[:, :, idx]
    return outb


def make_in_maps(cfg, inputs, n_batch, halves=2):
    D, E, KD, KE, NH, T, NT = cfg.D, cfg.E, cfg.KD, cfg.KE, cfg.NH, cfg.T, cfg.NT
    x = np.asarray(inputs["x"], np.float32)
    S = x.shape[1]
    own = NT * T
    assert S == halves * own
    pm, pstart = pool_matrices()
    biasn = near_bias(np.asarray(inputs["b_rel_bias"], np.float32), NH)
    shared = {
        "ada_w": np.ascontiguousarray(inputs["ada_w"], np.float32),
        "kv_ada_w": np.ascontiguousarray(inputs["kv_ada_w"], np.float32),
        "a_w_in": np.ascontiguousarray(inputs["a_w_in"], np.float32),
        "a_w_group": np.ascontiguousarray(inputs["a_w_group"], np.float32),
        "a_w_out": np.ascontiguousarray(inputs["a_w_out"], np.float32),
        "w_kv": np.ascontiguousarray(inputs["w_kv"], np.float32),
        "b_w_in": np.ascontiguousarray(inputs["b_w_in"], np.float32),
        "b_w_out": np.ascontiguousarray(inputs["b_w_out"], np.float32),
        "biasn": biasn,
        "pm_std": pm,
        "identm": np.eye(128, dtype=np.float32),
    }
    cf = np.asarray(inputs["b_rel_bias"], np.float32)[:, :, 256].reshape(1, 2 * NH)
    shared["cfar"] = np.ascontiguousarray(np.broadcast_to(cf, (128, 2 * NH)))
    in_maps = []
    for b in range(n_batch):
        vec_rows = [
            np.asarray(inputs["ada_b"], np.float32).reshape(-1, 128),
            np.asarray(inputs["kv_ada_b"], np.float32).reshape(-1, 128),
            np.asarray(inputs["norm_g"], np.float32).reshape(-1, 128),
            np.asarray(inputs["kv_norm_g"], np.float32).reshape(-1, 128),
            np.asarray(inputs["final_g"], np.float32).reshape(-1, 128),
            np.asarray(inputs["a_scale"], np.float32).reshape(-1, 128),
            np.asarray(inputs["c"], np.float32)[b].reshape(-1, 128),
        ]
        vecs = np.ascontiguousarray(np.concatenate(vec_rows, axis=0))
        assert vecs.shape[0] == cfg.NV
        for hf in range(halves):
            t0 = hf * own
            xin = np.zeros((cfg.NTOK, D), np.float32)
            lo = t0 - (128 + T)
            src_lo = max(lo, 0)
            xin[src_lo - lo:, :] = x[b, src_lo:t0 + own, :]
            m = dict(shared)
            m["xin"] = xin
            m["vecs"] = vecs
            pm0 = np.zeros((128, 4, PW + 15), np.float32)
            if t0 == 0:
                pm0[:, :, :PW] = pstart
                hm = np.full((128, 1), NEG, np.float32)
            else:
                pm0[:, :, :PW] = pm
                pm0[:, :, PW:] = pm[:, :, 128:PW]
                hm = np.zeros((128, 1), np.float32)
            m["pm0"] = pm0
            m["hmask"] = hm
            in_maps.append(m)
    return in_maps


_NC_CACHE = {}


def kernel(**inputs):
    cfg = Cfg()
    if "nc" not in _NC_CACHE:
        _NC_CACHE["nc"] = build_program(cfg)
    nc = _NC_CACHE["nc"]
    B = inputs["x"].shape[0]
    in_maps = make_in_maps(cfg, inputs, B)
    res = run_bass_kernel_spmd(nc, in_maps, core_ids=list(range(len(in_maps))))
    own = cfg.NT * cfg.T
    outp = np.empty((B, 2 * own, cfg.D), np.float32)
    i = 0
    for b in range(B):
        for hf in range(2):
            outp[b, hf * own:(hf + 1) * own, :] = res.results[i]["out"]
            i += 1
    return outp
```

```python
import numpy as np
from contextlib import ExitStack
import concourse.bass as bass
import concourse.mybir as mybir
from concourse.bass_utils import run_bass_kernel_spmd

F32 = mybir.dt.float32
BF16 = mybir.dt.bfloat16
AF = mybir.ActivationFunctionType
ALU = mybir.AluOpType

EPS = 1e-6
NEG = -30000.0
POOL_WINDOWS = (2, 4, 8, 16)
PW = 143


class Cfg:
    def __init__(s, D=1024, E=2048, NT=8, T=512, WB=3):
        s.D, s.E, s.NT, s.T, s.WB = D, E, NT, T, WB
        s.KD = D // 128
        s.KE = E // 128
        s.NH = E // 128
        s.GW = E // 4
        s.CPG = s.GW // 128
        s.NB = T // 128
        s.NTOK = 128 + T + NT * T
        s.ZW = min(512, E)
        s.OW = min(256, D)
        r = 0
        s.r_adab = r; r += 4 * 3 * s.KD
        s.r_kvb = r; r += 2 * s.KD
        s.r_ng = r; r += 4 * s.KD
        s.r_kvg = r; r += s.KD
        s.r_fg = r; r += s.KD
        s.r_asc = r; r += 2 * s.KE
        s.r_c = r; r += s.KD
        s.NV = r
        s.NM = 4 * 3 * s.KD + 2 * s.KD


class Op:
    __slots__ = ("eng", "fn", "idx", "deps", "dma_slot", "dma_val", "seq",
                 "sig", "sigval", "waits", "vc")


class Prog:
    ENG = ("pe", "act", "dve", "pool", "sp")

    def __init__(self):
        self.ops = []
        self.lastw = {}
        self.readers = {}
        self.dma_cnt = {}
        self.group_slots = set()

    def add(self, eng, fn, reads=(), writes=(), dma_slot=None):
        o = Op()
        o.eng, o.fn, o.idx, o.dma_slot = eng, fn, len(self.ops), dma_slot
        o.sig = False
        o.dma_val = 0
        deps = {}
        for k in reads:
            w = self.lastw.get(k)
            if w is not None:
                deps[w.idx] = w
        for k in writes:
            w = self.lastw.get(k)
            if w is not None:
                deps[w.idx] = w
            for r in self.readers.get(k, ()):
                deps[r.idx] = r
        for k in reads:
            self.readers.setdefault(k, []).append(o)
        for k in writes:
            self.lastw[k] = o
            self.readers[k] = []
        o.deps = list(deps.values())
        if dma_slot is not None:
            c = self.dma_cnt.get(dma_slot, 0) + 1
            self.dma_cnt[dma_slot] = c
            o.dma_val = 16 * c
        self.ops.append(o)
        return o

    def finalize(self):
        seq = {e: 0 for e in self.ENG}
        for o in self.ops:
            if o.dma_slot is None:
                seq[o.eng] += 1
                o.seq = seq[o.eng]
        K = {e: {} for e in self.ENG}
        for o in self.ops:
            Ke = K[o.eng]
            cand = []
            for d in o.deps:
                if d.dma_slot is not None:
                    key = ("dma", d.dma_slot)
                    val = d.dma_val
                    if d.dma_slot in self.group_slots:
                        val = 16 * self.dma_cnt[d.dma_slot]
                else:
                    if d.eng == "pe" and o.eng == "pe" and o.dma_slot is None:
                        continue
                    key = ("eng", d.eng)
                    val = d.seq
                cand.append((d.idx, key, val, d))
            cand.sort(key=lambda t: -t[0])
            waits = []
            for _, key, val, d in cand:
                if Ke.get(key, 0) >= val:
                    continue
                if key[0] == "eng":
                    d.sig = True
                waits.append((key, val, d))
                for k2, v2 in d.vc.items():
                    if Ke.get(k2, 0) < v2:
                        Ke[k2] = v2
                if Ke.get(key, 0) < val:
                    Ke[key] = val
            o.waits = waits
            vc = dict(Ke)
            if o.dma_slot is None:
                vc[("eng", o.eng)] = o.seq
            else:
                key = ("dma", o.dma_slot)
                val = o.dma_val
                if o.dma_slot in self.group_slots:
                    val = 16 * self.dma_cnt[o.dma_slot]
                if vc.get(key, 0) < val:
                    vc[key] = val
            o.vc = vc
        cnt = {e: 0 for e in self.ENG}
        for o in self.ops:
            if o.dma_slot is None and o.sig:
                cnt[o.eng] += 1
                o.sigval = cnt[o.eng]
        self.sig_counts = cnt

    def emit(self, nc, es):
        self.finalize()
        sems = {}
        for e in ("pe", "act", "dve", "pool"):
            sems[("eng", e)] = es.enter_context(nc.semaphore("s_" + e))
        for slot in self.dma_cnt:
            sems[("dma", slot)] = es.enter_context(nc.semaphore("d_" + str(slot)))
        block = es.enter_context(nc.Block())
        per = {e: [o for o in self.ops if o.eng == e] for e in self.ENG}

        def run(eng_name, eng):
            for o in per[eng_name]:
                ws_ = [(sems[key], d.sigval if key[0] == "eng" else val)
                       for key, val, d in o.waits]
                emb = None
                if ws_ and o.dma_slot is None:
                    emb = ws_.pop()
                for sm, v in ws_:
                    eng.wait_ge(sm, v)
                inst = o.fn(eng)
                if emb is not None:
                    inst._wait_ge(emb[0], emb[1])
                if o.dma_slot is not None:
                    inst.then_inc(sems[("dma", o.dma_slot)], 16)
                elif o.sig:
                    inst.then_inc(sems[("eng", o.eng)], 1)

        @block.tensor
        def _(e):
            run("pe", e)

        @block.scalar
        def _(e):
            run("act", e)

        @block.vector
        def _(e):
            run("dve", e)

        @block.gpsimd
        def _(e):
            run("pool", e)

        @block.sync
        def _(e):
            run("sp", e)


def weight_schedule(cfg):
    KD, KE = cfg.KD, cfg.KE
    S = []

    def ada(l):
        cols = 3 * cfg.D
        pw = 512 if cols % 512 == 0 else 256
        for c in range(cols // pw):
            S.append(("ada", l, c, pw))

    def kvada():
        cols = 2 * cfg.D
        pw = 512 if cols % 512 == 0 else 256
        for c in range(cols // pw):
            S.append(("kvada", 0, c, pw))

    for l in range(4):
        ada(l)
    kvada()

    def alayer(a, only_val):
        if not only_val:
            for c in range(cfg.E // cfg.ZW):
                S.append(("a_z", a, c, cfg.ZW))
        for g in range(4):
            S.append(("a_val", a, g, cfg.GW))
            if not only_val:
                S.append(("a_wg", a, g, cfg.GW))
        if not only_val:
            for c in range(cfg.D // cfg.OW):
                S.append(("a_wo", a, c, cfg.OW))

    def kv():
        for c in range(cfg.E // cfg.ZW):
            S.append(("kv_k", 0, c, cfg.ZW))
        for c in range(cfg.E // cfg.ZW):
            S.append(("kv_v", 0, c, cfg.ZW))

    def blayer(b):
        npc = cfg.E // cfg.ZW
        for c in range(npc):
            S.append(("b_z", b, c, cfg.ZW))
        for c in range(npc):
            S.append(("b_q", b, c, cfg.ZW))
        for c in range(cfg.D // cfg.OW):
            S.append(("b_wo", b, c, cfg.OW))

    alayer(0, False)
    alayer(1, True)
    alayer(0, False)
    alayer(1, False)
    kv()
    for _ in range(cfg.NT):
        alayer(0, False)
        alayer(1, False)
        kv()
        blayer(0)
        blayer(1)
    return S


def build_program(cfg):
    D, E, KD, KE, NH, GW, CPG, T, NT = (cfg.D, cfg.E, cfg.KD, cfg.KE, cfg.NH,
                                        cfg.GW, cfg.CPG, cfg.T, cfg.NT)
    nc = bass.Bass("TRN2", target_bir_lowering=False)
    es = ExitStack()

    def dram(name, shape, kind="ExternalInput"):
        return nc.dram_tensor(name, list(shape), F32, kind=kind).ap()

    xin = dram("xin", [cfg.NTOK, D])
    vecs = dram("vecs", [cfg.NV, 128])
    ada_w = dram("ada_w", [4, D, 3 * D])
    kv_ada_w = dram("kv_ada_w", [D, 2 * D])
    a_w_in = dram("a_w_in", [2, D, 2 * E])
    a_w_group = dram("a_w_group", [2, 4, GW, GW])
    a_w_out = dram("a_w_out", [2, E, D])
    w_kv = dram("w_kv", [D, 2 * E])
    b_w_in = dram("b_w_in", [2, D, 2 * E])
    b_w_out = dram("b_w_out", [2, E, D])
    biasn = dram("biasn", [2, NH, 128, 256])
    cfar = dram("cfar", [128, 2 * NH])
    hmask = dram("hmask", [128, 1])
    pm_std = dram("pm_std", [128, 4, PW])
    pm0 = dram("pm0", [128, 4, PW + 15])
    identm = dram("identm", [128, 128])
    out = dram("out", [NT * T, D], kind="ExternalOutput")
    escr = nc.dram_tensor("escr", [2, 128, NH * 256], BF16, kind="Internal").ap()

    def sb(name, shape, dt):
        return es.enter_context(nc.sbuf_tensor(name, list(shape), dt))

    def psb(name):
        return es.enter_context(nc.psum_tensor(name, [128, 512], F32))

    h = sb("h", [128, KD, T], F32)
    u = sb("u", [128, KD, T], BF16)
    kring = sb("kring", [128, NH, 2 * T], BF16)
    vring = sb("vring", [128, 2 * cfg.NB, E], BF16)
    hist = sb("hist", [128, 2, E], BF16)
    sz = sb("sz", [128, KE, T], BF16)
    bufq = sb("bufq", [128, KE * T], BF16)
    wbuf = [sb(f"wbuf{i}", [128, 4096], BF16) for i in range(cfg.WB)]
    enear = sb("enear", [128, NH, 256], BF16)
    xst = [sb(f"xst{i}", [128, D], F32) for i in range(2)]
    ost = [sb(f"ost{i}", [128, 512], F32) for i in range(4)]
    NSQ = 4
    sq = [sb(f"sq{i}", [128, T], BF16) for i in range(NSQ)]
    rt = sb("rt", [128, T], F32)
    rstd = sb("rstd", [128, T], F32)
    tmpf = [sb(f"tmpf{i}", [128, T], F32) for i in range(2)]
    rd = [sb(f"rd{i}", [128, T], F32) for i in range(2)]
    NPT = 6
    LOOK = 3
    pT = [sb(f"pT{i}", [128, 512], BF16) for i in range(NPT)]
    vst = sb("vst", [128, 128], F32)
    vecT = sb("vecT", [128, cfg.NV], F32)
    modT = sb("modT", [128, cfg.NM], F32)
    gm = sb("gm", [128, 5 * KD], F32)
    cact = sb("cact", [128, KD], BF16)
    cfar_t = sb("cfar_t", [128, 2 * NH], F32)
    cfsc = sb("cfsc", [128, 2 * NH], F32)
    cneg = sb("cneg", [128, 2 * NH], F32)
    chalo = sb("chalo", [128, 2 * NH], F32)
    hm_t = sb("hm_t", [128, 1], F32)
    pms = sb("pms", [128, 4, PW], BF16)
    pm0s = sb("pm0s", [128, 4, PW + 15], BF16)
    ident = sb("ident", [128, 128], F32)
    ones = sb("ones", [128, 128], BF16)
    epsb = sb("epsb", [128, 1], F32)
    fcell = sb("fcell", [128, 2], F32)

    banks = [psb(f"ps{i}") for i in range(8)]

    P = Prog()
    P.group_slots.add("const_sp")
    P.group_slots.add("const_pool")

    rr = {"a": 0, "s": 0, "evac": 0, "pt": 0, "xst": 0, "ost": 0, "bst": 0, "sq": 0,
          "tmpf": 0}

    def nextbank():
        b = rr["a"] % 4
        rr["a"] += 1
        return b

    def mm(bank, out_ap, lhsT, rhs, start, reads, skip=False):
        def f(e, out_ap=out_ap, lhsT=lhsT, rhs=rhs, start=start, skip=skip):
            return e.matmul(out_ap, lhsT, rhs, start=start, stop=True,
                            skip_group_check=True)
        return P.add("pe", f, reads=reads, writes=[("ps", bank)])

    def act(out_ap, in_ap, func, reads, writes, bias=None, scale=None):
        def f(e):
            kw = {}
            if bias is not None:
                kw["bias"] = bias
            if scale is not None:
                kw["scale"] = scale
            return e.activation(out=out_ap, in_=in_ap, func=func, **kw)
        return P.add("act", f, reads=reads, writes=writes)

    def dve(fn, reads, writes):
        return P.add("dve", fn, reads=reads, writes=writes)

    def evac(out_ap, in_ap, reads, writes, prefer=None):
        rr["evac"] += 1
        if prefer == "act" or (prefer is None and rr["evac"] % 2 == 0):
            return act(out_ap, in_ap, AF.Copy, reads, writes)
        return dve(lambda e: e.tensor_copy(out=out_ap, in_=in_ap), reads, writes)

    def dma(queue, out_ap, in_ap, slot, reads, writes):
        return P.add(queue, lambda e: e.dma_start(out=out_ap, in_=in_ap),
                     reads=reads, writes=writes, dma_slot=slot)

    sched = weight_schedule(cfg)
    ws = {"next_load": 0, "next_use": 0}

    def piece_src(desc):
        kind, l, c, pw = desc
        if kind == "ada":
            src, kc = ada_w[l, :, c * pw:(c + 1) * pw], KD
        elif kind == "kvada":
            src, kc = kv_ada_w[:, c * pw:(c + 1) * pw], KD
        elif kind == "a_z":
            src, kc = a_w_in[l, :, E + c * pw:E + (c + 1) * pw], KD
        elif kind == "a_val":
            src, kc = a_w_in[l, :, c * pw:(c + 1) * pw], KD
        elif kind == "a_wg":
            src, kc = a_w_group[l, c, :, :], CPG
        elif kind == "a_wo":
            src, kc = a_w_out[l, :, c * pw:(c + 1) * pw], KE
        elif kind == "kv_k":
            src, kc = w_kv[:, c * pw:(c + 1) * pw], KD
        elif kind == "kv_v":
            src, kc = w_kv[:, E + c * pw:E + (c + 1) * pw], KD
        elif kind == "b_q":
            src, kc = b_w_in[l, :, c * pw:(c + 1) * pw], KD
        elif kind == "b_z":
            src, kc = b_w_in[l, :, E + c * pw:E + (c + 1) * pw], KD
        elif kind == "b_wo":
            src, kc = b_w_out[l, :, c * pw:(c + 1) * pw], KE
        else:
            raise ValueError(kind)
        return src.rearrange("(kc p) n -> p kc n", p=128), kc, pw

    def issue_load():
        i = ws["next_load"]
        if i >= len(sched):
            return
        ws["next_load"] += 1
        slot = i % cfg.WB
        src, kc, pw = piece_src(sched[i])
        dst = wbuf[slot][:, 0:kc * pw].rearrange("p (kc n) -> p kc n", kc=kc)
        dma("pool", dst, src, f"w{slot}", reads=[], writes=[("wbuf", slot)])

    def get_piece(kind, l, c):
        i = ws["next_use"]
        assert sched[i][:3] == (kind, l, c), (sched[i], kind, l, c)
        ws["next_use"] += 1
        slot = i % cfg.WB
        _, kc, pw = piece_src(sched[i])
        view = wbuf[slot][:, 0:kc * pw].rearrange("p (kc n) -> p kc n", kc=kc)
        return view, ("wbuf", slot)

    def release_piece():
        issue_load()

    P.add("dve", lambda e: e.memset(ones[:], 1.0), writes=[("ones",)])
    P.add("dve", lambda e: e.memset(epsb[:], EPS), writes=[("epsb",)])
    P.add("dve", lambda e: e.memset(hist[:].rearrange("p a e -> p (a e)"), 0.0),
          writes=[("hist", 0), ("hist", 1)])
    dma("sp", ident[:], identm[:, :], "const_sp", [], [("ident",)])
    dma("sp", cfar_t[:], cfar[:, :], "const_sp", [], [("cfar",)])
    dma("sp", hm_t[:], hmask[:, :], "const_sp", [], [("hm",)])
    dma("pool", pms[:], pm_std[:, :, :], "const_pool", [], [("pms",)])
    dma("pool", pm0s[:], pm0[:, :, :], "const_pool", [], [("pm0s",)])
    for i in range(cfg.WB):
        issue_load()

    r0 = 0
    while r0 < cfg.NV:
        n = min(128, cfg.NV - r0)
        dma("sp", vst[0:n, :], vecs[r0:r0 + n, :], "vst", [], [("vst",)])
        b = nextbank()
        P.add("pe", lambda e, n=n, b=b: e.transpose(
            banks[b][:, 0:n], vst[0:n, :], ident[0:n, 0:n]),
            reads=[("vst",), ("ident",)], writes=[("ps", b)])
        dve(lambda e, n=n, b=b, r0=r0: e.tensor_copy(
            out=vecT[:, r0:r0 + n], in_=banks[b][:, 0:n]),
            reads=[("ps", b)], writes=[("vecT",)])
        r0 += n

    act(cact[:], vecT[:, cfg.r_c:cfg.r_c + KD], AF.Silu, [("vecT",)], [("cact",)])
    dve(lambda e: e.tensor_copy(out=cfsc[:], in_=cfar_t[:]), [("cfar",)], [("cfsc",)])
    dve(lambda e: e.tensor_scalar(out=cneg[:], in0=cfar_t[:], scalar1=-1.0,
                                  scalar2=None, op0=ALU.mult),
        [("cfar",)], [("cneg",)])
    dve(lambda e: e.tensor_scalar(out=chalo[:], in0=cfar_t[:], scalar1=hm_t[:, 0:1],
                                  scalar2=None, op0=ALU.add),
        [("cfar",), ("hm",)], [("chalo",)])

    h_flat = h[:].rearrange("p k t -> p (k t)")
    sz_flat = sz[:].rearrange("p k t -> p (k t)")
    assert KD * T == NH * 256 and KE * T == 2 * NH * 256
    hkeys = [("h", k) for k in range(KD)]
    szkeys = [("sz", k) for k in range(KE)]
    for bi_ in range(2):
        dma("sp", h_flat.rearrange("p (h c) -> p h c", h=NH),
            biasn[bi_, :, :, :].rearrange("h p c -> p h c"), "bin", [], hkeys)
        for hh in range(NH):
            o0 = bi_ * NH * 256 + hh * 256
            act(sz_flat[:, o0:o0 + 256], h_flat[:, hh * 256:(hh + 1) * 256], AF.Exp,
                hkeys + [("cneg",)], szkeys,
                bias=cneg[:, bi_ * NH + hh:bi_ * NH + hh + 1], scale=1.0)
        dma("sp", escr[bi_, :, :], sz_flat[:, bi_ * NH * 256:(bi_ + 1) * NH * 256], "bout",
            szkeys, [("escr", bi_)])

    def load_enear(bi_):
        dma("sp", enear[:].rearrange("p h c -> p (h c)"), escr[bi_, :, :], "en",
            [("escr", bi_)], [("enear", hh) for hh in range(NH)])

    mb = 7
    first = True
    for desc in [d for d in sched if d[0] in ("ada", "kvada")]:
        kind, l, c, pw = desc
        wv, wkey = get_piece(kind, l, c)
        base = (l * 3 * KD if kind == "ada" else 4 * 3 * KD) + c * (pw // 128)
        for j in range(pw // 128):
            col = base + j
            for kc in range(KD):
                mm(mb, banks[mb][:, col:col + 1], wv[:, kc, j * 128:(j + 1) * 128],
                   cact[:, kc:kc + 1], start=(kc == 0),
                   reads=[wkey, ("cact",)], skip=True)
        release_piece()
    dve(lambda e: e.tensor_tensor(out=modT[:], in0=banks[mb][:, 0:cfg.NM],
                                  in1=vecT[:, cfg.r_adab:cfg.r_adab + cfg.NM],
                                  op=ALU.add),
        [("ps", mb), ("vecT",)], [("modT",)])
    for l in range(4):
        sc0 = l * 3 * KD + KD
        dve(lambda e, l=l, sc0=sc0: e.scalar_tensor_tensor(
            out=gm[:, l * KD:(l + 1) * KD], in0=modT[:, sc0:sc0 + KD], scalar=1.0,
            in1=vecT[:, cfg.r_ng + l * KD:cfg.r_ng + (l + 1) * KD],
            op0=ALU.add, op1=ALU.mult), [("modT",), ("vecT",)], [("gm",)])
    kvb = 4 * 3 * KD
    dve(lambda e: e.scalar_tensor_tensor(
        out=gm[:, 4 * KD:5 * KD], in0=modT[:, kvb + KD:kvb + 2 * KD], scalar=1.0,
        in1=vecT[:, cfg.r_kvg:cfg.r_kvg + KD], op0=ALU.add, op1=ALU.mult),
        [("modT",), ("vecT",)], [("gm",)])

    def shift_col(l, k):
        return modT[:, l * 3 * KD + k:l * 3 * KD + k + 1]

    def gate_col(l, k):
        return modT[:, l * 3 * KD + 2 * KD + k:l * 3 * KD + 2 * KD + k + 1]

    xq = {"issued": {}}

    def issue_x(tok0, tb):
        if (tok0, tb) in xq["issued"]:
            return
        s = rr["xst"] % 2
        rr["xst"] += 1
        dma("sp", xst[s][:], xin[tok0 + tb * 128:tok0 + (tb + 1) * 128, :],
            f"x{s}", [], [("xst", s)])
        xq["issued"][(tok0, tb)] = s

    def load_x(tok0, nb, Tt, nxt=None):
        for tb in range(nb):
            issue_x(tok0, tb)
            if tb + 1 < nb:
                issue_x(tok0, tb + 1)
            s = xq["issued"][(tok0, tb)]
            for k0 in range(0, KD, 4):
                nk = min(4, KD - k0)
                b = nextbank()
                for kk in range(nk):
                    k = k0 + kk
                    P.add("pe", lambda e, b=b, kk=kk, k=k, s=s: e.transpose(
                        banks[b][:, kk * 128:(kk + 1) * 128],
                        xst[s][:, k * 128:(k + 1) * 128], ident[:]),
                        reads=[("xst", s), ("ident",)], writes=[("ps", b)])
                evac(h[:, k0:k0 + nk, tb * 128:(tb + 1) * 128],
                     banks[b][:, 0:nk * 128].rearrange("p (k t) -> p k t", k=nk),
                     [("ps", b)], [("h", k) for k in range(k0, k0 + nk)])
        if nxt is not None:
            issue_x(nxt[0], 0)
            if nxt[1] > 1:
                issue_x(nxt[0], 1)

    SB_ = 7
    st = {"ready": False, "nmm": 0}

    def stats_square(k, Tt):
        s_ = rr["sq"] % NSQ
        rr["sq"] += 1
        act(sq[s_][:, 0:Tt], h[:, k, 0:Tt], AF.Square, [("h", k)], [("sq", s_)])
        return s_

    def stats_mm(s_, Tt):
        mm(SB_, banks[SB_][:, 0:Tt], ones[:], sq[s_][:, 0:Tt], start=(st["nmm"] == 0),
           reads=[("ones",), ("sq", s_)])
        st["nmm"] += 1

    def stats_finish(Tt):
        assert st["nmm"] == KD
        st["nmm"] = 0
        act(rt[:, 0:Tt], banks[SB_][:, 0:Tt], AF.Ln, [("ps", SB_), ("epsb",)], [("rt",)],
            bias=epsb[:, 0:1], scale=1.0 / D)
        act(rstd[:, 0:Tt], rt[:, 0:Tt], AF.Exp, [("rt",)], [("rstd",)], scale=-0.5)
        st["ready"] = True

    def rms_stats(Tt):
        if st["ready"]:
            st["ready"] = False
            return
        pend = []
        for k in range(KD):
            pend.append(stats_square(k, Tt))
            if len(pend) > 1:
                stats_mm(pend.pop(0), Tt)
        for s_ in pend:
            stats_mm(s_, Tt)
        stats_finish(Tt)
        st["ready"] = False

    def modulate(Tt, gcol0, shift_fn):
        for k in range(KD):
            s = rr["tmpf"] % 2
            rr["tmpf"] += 1
            dve(lambda e, k=k, s=s: e.tensor_tensor(
                out=tmpf[s][:, 0:Tt], in0=h[:, k, 0:Tt], in1=rstd[:, 0:Tt], op=ALU.mult),
                [("h", k), ("rstd",)], [("tmpf", s)])
            act(u[:, k, 0:Tt], tmpf[s][:, 0:Tt], AF.Identity,
                [("tmpf", s), ("gm",), ("modT",)], [("u", k)],
                bias=shift_fn(k), scale=gm[:, gcol0 + k:gcol0 + k + 1])

    def proj_fm(kind, l, width, Tt, consume):
        npieces = E // width
        for c in range(npieces):
            wv, wkey = get_piece(kind, l, c)
            nj = width // 128
            if c == 0 and nj <= 4:
                bs = [nextbank() for _ in range(nj)]
                for kc in range(KD):
                    for j in range(nj):
                        mm(bs[j], banks[bs[j]][:, 0:Tt], wv[:, kc, j * 128:(j + 1) * 128],
                           u[:, kc, 0:Tt], start=(kc == 0), reads=[wkey, ("u", kc)])
                for j in range(nj):
                    consume(c * nj + j, bs[j])
            else:
                for j in range(nj):
                    oc = c * nj + j
                    b = nextbank()
                    for kc in range(KD):
                        mm(b, banks[b][:, 0:Tt], wv[:, kc, j * 128:(j + 1) * 128],
                           u[:, kc, 0:Tt], start=(kc == 0),
                           reads=[wkey, ("u", kc)])
                    consume(oc, b)
            release_piece()

    def out_proj(kind, l, layer, Tt):
        pend = []
        for c in range(D // cfg.OW):
            wv, wkey = get_piece(kind, l, c)
            for j in range(cfg.OW // 128):
                dc = c * (cfg.OW // 128) + j
                b = nextbank()
                for ec in range(KE):
                    mm(b, banks[b][:, 0:Tt], wv[:, ec, j * 128:(j + 1) * 128],
                       sz[:, ec, 0:Tt], start=(ec == 0),
                       reads=[wkey, ("sz", ec)])
                dve(lambda e, b=b, dc=dc: e.scalar_tensor_tensor(
                    out=h[:, dc, 0:Tt], in0=banks[b][:, 0:Tt], scalar=gate_col(layer, dc),
                    in1=h[:, dc, 0:Tt], op0=ALU.mult, op1=ALU.add),
                    [("ps", b), ("modT",), ("h", dc)], [("h", dc)])
                pend.append(stats_square(dc, Tt))
                if len(pend) > 2:
                    stats_mm(pend.pop(0), Tt)
            release_piece()
        for s_ in pend:
            stats_mm(s_, Tt)
        stats_finish(Tt)

    val_v = [bufq[:, i * cfg.NB * GW:(i + 1) * cfg.NB * GW].rearrange(
        "p (b n) -> p b n", b=cfg.NB) for i in range(2)]
    off = 2 * cfg.NB * GW
    pooled_v = [bufq[:, off + i * CPG * T: off + (i + 1) * CPG * T].rearrange(
        "p (c t) -> p c t", c=CPG) for i in range(2)]
    q_v = bufq[:, 0:KE * T].rearrange("p (h t) -> p h t", h=KE)

    BQ = ("bufq_mode",)

    def fence():
        dve(lambda e: e.memset(fcell[:, 0:1], 0.0), [], [BQ])

    def a_layer(a, Tt, only_val, first_real):
        layer = a
        nb = Tt // 128
        if a == 0:
            fence()
        rms_stats(Tt)
        modulate(Tt, layer * KD, lambda k: shift_col(layer, k))
        if not only_val:
            def z_consume(oc, b):
                act(sz[:, oc, 0:Tt], banks[b][:, 0:Tt], AF.Silu, [("ps", b)], [("sz", oc)])
            proj_fm("a_z", a, cfg.ZW, Tt, z_consume)
        for g in range(4):
            vs = g % 2
            wv, wkey = get_piece("a_val", a, g)
            for tb in range(nb):
                b = nextbank()
                for kc in range(KD):
                    mm(b, banks[b][:, 0:GW], u[:, kc, tb * 128:(tb + 1) * 128],
                       wv[:, kc, 0:GW], start=(kc == 0), reads=[wkey, ("u", kc)])
                evac(val_v[vs][:, tb, 0:GW], banks[b][:, 0:GW], [("ps", b), BQ],
                     [("val", vs, tb)])
            release_piece()
            if not only_val:
                wg, wgkey = get_piece("a_wg", a, g)
                for ci in range(CPG):
                    ec = g * CPG + ci
                    b = nextbank()
                    if first_real:
                        ph = pm0s[:, g, PW:PW + 15]
                    else:
                        ph = pms[:, g, 128:PW]
                    mm(b, banks[b][:, 0:15], hist[:, a, ec * 128:(ec + 1) * 128], ph,
                       start=True, reads=[("hist", a), ("pms",), ("pm0s",)], skip=True)
                    for sbk in range(nb):
                        ncols = min(PW, Tt - 128 * sbk)
                        if first_real and sbk == 0:
                            pr = pm0s[:, g, 0:ncols]
                        else:
                            pr = pms[:, g, 0:ncols]
                        mm(b, banks[b][:, 128 * sbk:128 * sbk + ncols],
                           val_v[vs][:, sbk, ci * 128:(ci + 1) * 128], pr, start=False,
                           reads=[("val", vs, sbk), ("pms",), ("pm0s",), BQ], skip=True)
                    evac(pooled_v[vs][:, ci, 0:Tt], banks[b][:, 0:Tt], [("ps", b), BQ],
                         [("pooled", vs, ci)])
            dve(lambda e, vs=vs, g=g: e.tensor_copy(
                out=hist[:, a, g * GW:(g + 1) * GW], in_=val_v[vs][:, nb - 1, 0:GW]),
                [("val", vs, nb - 1), BQ], [("hist", a)])
            if not only_val:
                for co in range(CPG):
                    oc = g * CPG + co
                    b = nextbank()
                    for ci in range(CPG):
                        mm(b, banks[b][:, 0:Tt], wg[:, ci, co * 128:(co + 1) * 128],
                           pooled_v[vs][:, ci, 0:Tt], start=(ci == 0),
                           reads=[wgkey, ("pooled", vs, ci), BQ])
                    asc = vecT[:, cfg.r_asc + a * KE + oc:cfg.r_asc + a * KE + oc + 1]
                    dve(lambda e, b=b, oc=oc, asc=asc: e.scalar_tensor_tensor(
                        out=sz[:, oc, 0:Tt], in0=banks[b][:, 0:Tt], scalar=asc,
                        in1=sz[:, oc, 0:Tt], op0=ALU.mult, op1=ALU.mult),
                        [("ps", b), ("vecT",), ("sz", oc)], [("sz", oc)])
                release_piece()
        if not only_val:
            out_proj("a_wo", a, layer, Tt)

    def kv_layer(slot):
        Tt = T
        rms_stats(Tt)
        kvb_ = 4 * 3 * KD
        modulate(Tt, 4 * KD, lambda k: modT[:, kvb_ + k:kvb_ + k + 1])

        def k_consume(oc, b):
            evac(kring[:, oc, slot * T:(slot + 1) * T], banks[b][:, 0:Tt], [("ps", b)],
                 [("kring", slot, oc)])
        proj_fm("kv_k", 0, cfg.ZW, Tt, k_consume)
        for c in range(E // cfg.ZW):
            wv, wkey = get_piece("kv_v", 0, c)
            for tb in range(cfg.NB):
                b = nextbank()
                for kc in range(KD):
                    mm(b, banks[b][:, 0:cfg.ZW], u[:, kc, tb * 128:(tb + 1) * 128],
                       wv[:, kc, 0:cfg.ZW], start=(kc == 0), reads=[wkey, ("u", kc)])
                evac(vring[:, slot * cfg.NB + tb, c * cfg.ZW:(c + 1) * cfg.ZW],
                     banks[b][:, 0:cfg.ZW], [("ps", b)], [("vring", slot * cfg.NB + tb, c)])
            release_piece()

    sm_scale = float(128 ** -0.5)

    def b_layer(bi, slot, first_real, reuse_stats):
        layer = 2 + bi
        Tt = T
        if bi == 0:
            fence()
        if not reuse_stats:
            rms_stats(Tt)
        modulate(Tt, layer * KD, lambda k: shift_col(layer, k))

        if first_real:
            groups = [[3], [4], [0, 2], [1], [6], [7, 5]]
        else:
            groups = [[3], [4], [0, 2], [1, 6], [7, 5]]
        NG = len(groups)
        steps = [(hh, gi) for hh in range(NH) for gi in range(NG)]
        info = {}

        def blk(j):
            a_lo = max(0, 2 * j - 8)
            a_hi = min(7, 2 * j + 1)
            return a_lo, a_hi, 64 * a_lo, 64 * (a_hi + 1)

        def stage1(idx):
            hh, gi = steps[idx]
            lh = bi * NH + hh
            sbk = rr["s"] % 3
            rr["s"] += 1
            ps_ = rr["pt"] % NPT
            rr["pt"] += 1
            off = 0
            members = []
            for n_, j in enumerate(groups[gi]):
                a_lo, a_hi, qlo, qhi = blk(j)
                ncols = qhi - qlo
                if j < 4:
                    kcol = (1 - slot) * T + j * 128
                    vblk = (1 - slot) * cfg.NB + j
                    kkey = ("kring", 1 - slot, hh)
                else:
                    kcol = slot * T + (j - 4) * 128
                    vblk = slot * cfg.NB + (j - 4)
                    kkey = ("kring", slot, hh)
                mm(sbk, banks[sbk][:, off:off + ncols], kring[:, hh, kcol:kcol + 128],
                   q_v[:, hh, qlo:qhi], start=(n_ == 0), reads=[kkey, ("q", hh), BQ])
                members.append((j, off, ncols, qlo, qhi, vblk, a_lo, a_hi))
                off += ncols
            total = off
            assert total <= 512
            if first_real and groups[gi][0] < 4:
                bcol = chalo[:, lh:lh + 1]
            else:
                bcol = cfsc[:, lh:lh + 1]
            act(pT[ps_][:, 0:total], banks[sbk][:, 0:total], AF.Exp,
                [("ps", sbk), ("cfsc",), ("chalo",)], [("pT", ps_)],
                bias=bcol, scale=sm_scale)
            for (j, off, ncols, qlo, qhi, vblk, a_lo, a_hi) in members:
                n0 = max(a_lo, 2 * j - 8)
                n1 = min(a_hi, 2 * j - 5)
                if n1 >= n0:
                    c0 = off + (n0 - a_lo) * 64
                    c1 = off + (n1 + 1 - a_lo) * 64
                    e0 = (n0 - (2 * j - 8)) * 64
                    e1 = (n1 + 1 - (2 * j - 8)) * 64
                    dve(lambda e, ps_=ps_, c0=c0, c1=c1, e0=e0, e1=e1, hh=hh: e.tensor_tensor(
                        out=pT[ps_][:, c0:c1], in0=pT[ps_][:, c0:c1],
                        in1=enear[:, hh, e0:e1], op=ALU.mult),
                        [("pT", ps_), ("enear", hh)], [("pT", ps_)])
                if 2 * j + 1 <= 7:
                    m0 = off + (2 * j + 1 - a_lo) * 64
                    dve(lambda e, ps_=ps_, m0=m0: e.memset(pT[ps_][0:64, m0:m0 + 64], 0.0),
                        [], [("pT", ps_)])
            info[idx] = (ps_, members)

        def stage2(idx):
            hh, gi = steps[idx]
            ps_, members = info.pop(idx)
            ob = 4 + (hh % 2)
            db = 6 + (hh % 2)
            for n_, (j, off, ncols, qlo, qhi, vblk, a_lo, a_hi) in enumerate(members):
                first = (gi == 0 and n_ == 0)
                vkeys = [("vring", vblk, c) for c in range(E // cfg.ZW)]
                mm(ob, banks[ob][:, qlo:qhi], vring[:, vblk, hh * 128:(hh + 1) * 128],
                   pT[ps_][:, off:off + ncols], start=first, reads=vkeys + [("pT", ps_)])
            for n_, (j, off, ncols, qlo, qhi, vblk, a_lo, a_hi) in enumerate(members):
                first = (gi == 0 and n_ == 0)
                mm(db, banks[db][:, qlo:qhi], ones[:], pT[ps_][:, off:off + ncols],
                   start=first, reads=[("ones",), ("pT", ps_)])
            if gi == NG - 1:
                r = hh % 2
                act(rd[r][:, 0:Tt], banks[db][:, 0:Tt], AF.Ln, [("ps", db)], [("rd", r)])
                act(rd[r][:, 0:Tt], rd[r][:, 0:Tt], AF.Exp, [("rd", r)], [("rd", r)],
                    scale=-1.0)
                dve(lambda e, r=r, hh=hh: e.tensor_tensor(
                    out=rd[r][:, 0:Tt], in0=rd[r][:, 0:Tt], in1=sz[:, hh, 0:Tt], op=ALU.mult),
                    [("rd", r), ("sz", hh)], [("rd", r)])
                dve(lambda e, r=r, hh=hh, ob=ob: e.tensor_tensor(
                    out=sz[:, hh, 0:Tt], in0=banks[ob][:, 0:Tt], in1=rd[r][:, 0:Tt],
                    op=ALU.mult), [("ps", ob), ("rd", r)], [("sz", hh)])

        HPP = cfg.ZW // 128
        npc = E // cfg.ZW
        pieces = {}
        PB = 3

        def proj_items(kind, hh, rotate=False):
            c, jj = divmod(hh, HPP)
            items = []
            bk = [PB]
            for kc in range(KD):
                def f(kc=kc):
                    if (kind, c) not in pieces:
                        pieces[(kind, c)] = get_piece(kind, bi, c)
                    wv, wkey = pieces[(kind, c)]
                    if kc == 0 and rotate:
                        bk[0] = nextbank()
                    pb = bk[0]
                    mm(pb, banks[pb][:, 0:Tt], wv[:, kc, jj * 128:(jj + 1) * 128],
                       u[:, kc, 0:Tt], start=(kc == 0), reads=[wkey, ("u", kc)])
                    if kc == KD - 1:
                        if kind == "b_q":
                            evac(q_v[:, hh, 0:Tt], banks[pb][:, 0:Tt], [("ps", pb), BQ],
                                 [("q", hh)], prefer="dve")
                        else:
                            act(sz[:, hh, 0:Tt], banks[pb][:, 0:Tt], AF.Silu,
                                [("ps", pb)], [("sz", hh)])
                        if jj == HPP - 1 or hh == NH - 1:
                            release_piece()
                items.append(f)
            return items

        nfirst = min(HPP, NH, 4)
        first_items = [proj_items("b_z", hh, rotate=True) for hh in range(nfirst)]
        for kc in range(KD):
            for hh in range(nfirst):
                first_items[hh][kc]()
        for hh in range(nfirst, NH):
            for f in proj_items("b_z", hh, rotate=True):
                f()
        for hh in range(min(HPP, NH)):
            for f in proj_items("b_q", hh, rotate=True):
                f()
        fifo = []
        for hh in range(HPP, NH):
            fifo.extend(proj_items("b_q", hh))
        fpos = [0]

        def do_work(idx):
            per_step = max(1, KD // 4)
            for _ in range(per_step):
                if fpos[0] < len(fifo):
                    fifo[fpos[0]]()
                    fpos[0] += 1

        def sbank():
            b = rr["s"] % 3
            rr["s"] += 1
            return b

        for idx in range(len(steps) + LOOK):
            if idx < len(steps):
                do_work(idx)
                stage1(idx)
            if idx >= LOOK:
                stage2(idx - LOOK)
        while fpos[0] < len(fifo):
            fifo[fpos[0]]()
            fpos[0] += 1
        if bi == 0:
            load_enear(1)
        out_proj("b_wo", bi, layer, Tt)

    NOST = 4

    def final_out(row0):
        Tt = T
        rms_stats(Tt)
        for k0 in range(0, KD, 4):
            nk = min(4, KD - k0)
            bks = [nextbank() for _ in range(cfg.NB)]
            for kk in range(nk):
                k = k0 + kk
                s_ = rr["tmpf"] % 2
                rr["tmpf"] += 1
                fg = vecT[:, cfg.r_fg + k:cfg.r_fg + k + 1]
                dve(lambda e, k=k, fg=fg, s_=s_: e.scalar_tensor_tensor(
                    out=tmpf[s_][:, 0:Tt], in0=h[:, k, 0:Tt], scalar=fg, in1=rstd[:, 0:Tt],
                    op0=ALU.mult, op1=ALU.mult),
                    [("h", k), ("vecT",), ("rstd",)], [("tmpf", s_)])
                for tb in range(cfg.NB):
                    b = bks[tb]
                    P.add("pe", lambda e, b=b, kk=kk, s_=s_, tb=tb: e.transpose(
                        banks[b][:, kk * 128:(kk + 1) * 128],
                        tmpf[s_][:, tb * 128:(tb + 1) * 128], ident[:]),
                        reads=[("tmpf", s_), ("ident",)], writes=[("ps", b)])
            for tb in range(cfg.NB):
                b = bks[tb]
                so = rr["ost"] % NOST
                rr["ost"] += 1
                evac(ost[so][:, 0:nk * 128], banks[b][:, 0:nk * 128], [("ps", b)],
                     [("ost", so)])
                dma("sp", out[row0 + tb * 128:row0 + (tb + 1) * 128, k0 * 128:(k0 + nk) * 128],
                    ost[so][:, 0:nk * 128], f"o{so}", [("ost", so)], [("outrow", so)])

    load_x(0, 1, 128, nxt=(128, cfg.NB))
    a_layer(0, 128, False, False)
    a_layer(1, 128, True, False)
    load_x(128, cfg.NB, T, nxt=(128 + T, cfg.NB))
    a_layer(0, T, False, False)
    a_layer(1, T, False, False)
    kv_layer(0)
    for it in range(NT):
        slot = (it + 1) % 2
        nxt = (128 + T + (it + 1) * T, cfg.NB) if it + 1 < NT else None
        load_x(128 + T + it * T, cfg.NB, T, nxt=nxt)
        load_enear(0)
        a_layer(0, T, False, it == 0)
        a_layer(1, T, False, it == 0)
        kv_layer(slot)
        b_layer(0, slot, it == 0, True)
        b_layer(1, slot, it == 0, False)
        final_out(it * T)
    assert ws["next_use"] == len(sched), (ws["next_use"], len(sched))

    last = P.add("sp", lambda e: e.nop(), reads=[("outrow", i) for i in range(4)])
    P.emit(nc, es)
    es.close()
    return nc


def pool_matrices():
    pm = np.zeros((128, 4, PW), np.float32)
    pstart = np.zeros((128, 4, PW), np.float32)
    for g, w in enumerate(POOL_WINDOWS):
        for t in range(PW):
            for s in range(max(0, t - w + 1), min(t, 127) + 1):
                pm[s, g, t] += 1.0 / w
                pstart[s, g, t] += 1.0 / min(t + 1, w)
            if t < 128:
                pm[t, g, t] -= 1.0
                pstart[t, g, t] -= 1.0
    return pm, pstart


def near_bias(rel_bias, NH):
    nl = rel_bias.shape[0]
    ki = np.arange(64)[:, None]
    qi = np.arange(64)[None, :]
    outb = np.full((nl, NH, 128, 256), NEG, np.float32)
    for i in range(2):
        for ap in range(4):
            delta = ap - i
            if delta < 0:
                continue
            idx = np.clip(64 * delta + qi - ki, -128, 128) + 128
            outb[:, :, i * 64:(i + 1) * 64, ap * 64:(ap + 1) * 64] = rel_bias[:, :, idx]
    return outb


def make_in_maps(cfg, inputs, n_batch, halves=2):
    D, E, KD, KE, NH, T, NT = cfg.D, cfg.E, cfg.KD, cfg.KE, cfg.NH, cfg.T, cfg.NT
    x = np.asarray(inputs["x"], np.float32)
    S = x.shape[1]
    own = NT * T
    assert S == halves * own
    pm, pstart = pool_matrices()
    biasn = near_bias(np.asarray(inputs["b_rel_bias"], np.float32), NH)
    shared = {
        "ada_w": np.ascontiguousarray(inputs["ada_w"], np.float32),
        "kv_ada_w": np.ascontiguousarray(inputs["kv_ada_w"], np.float32),
        "a_w_in": np.ascontiguousarray(inputs["a_w_in"], np.float32),
        "a_w_group": np.ascontiguousarray(inputs["a_w_group"], np.float32),
        "a_w_out": np.ascontiguousarray(inputs["a_w_out"], np.float32),
        "w_kv": np.ascontiguousarray(inputs["w_kv"], np.float32),
        "b_w_in": np.ascontiguousarray(inputs["b_w_in"], np.float32),
        "b_w_out": np.ascontiguousarray(inputs["b_w_out"], np.float32),
        "biasn": biasn,
        "pm_std": pm,
        "identm": np.eye(128, dtype=np.float32),
    }
    cf = np.asarray(inputs["b_rel_bias"], np.float32)[:, :, 256].reshape(1, 2 * NH)
    shared["cfar"] = np.ascontiguousarray(np.broadcast_to(cf, (128, 2 * NH)))
    in_maps = []
    for b in range(n_batch):
        vec_rows = [
            np.asarray(inputs["ada_b"], np.float32).reshape(-1, 128),
            np.asarray(inputs["kv_ada_b"], np.float32).reshape(-1, 128),
            np.asarray(inputs["norm_g"], np.float32).reshape(-1, 128),
            np.asarray(inputs["kv_norm_g"], np.float32).reshape(-1, 128),
            np.asarray(inputs["final_g"], np.float32).reshape(-1, 128),
            np.asarray(inputs["a_scale"], np.float32).reshape(-1, 128),
            np.asarray(inputs["c"], np.float32)[b].reshape(-1, 128),
        ]
        vecs = np.ascontiguousarray(np.concatenate(vec_rows, axis=0))
        assert vecs.shape[0] == cfg.NV
        for hf in range(halves):
            t0 = hf * own
            xin = np.zeros((cfg.NTOK, D), np.float32)
            lo = t0 - (128 + T)
            src_lo = max(lo, 0)
            xin[src_lo - lo:, :] = x[b, src_lo:t0 + own, :]
            m = dict(shared)
            m["xin"] = xin
            m["vecs"] = vecs
            pm0 = np.zeros((128, 4, PW + 15), np.float32)
            if t0 == 0:
                pm0[:, :, :PW] = pstart
                hm = np.full((128, 1), NEG, np.float32)
            else:
                pm0[:, :, :PW] = pm
                pm0[:, :, PW:] = pm[:, :, 128:PW]
                hm = np.zeros((128, 1), np.float32)
            m["pm0"] = pm0
            m["hmask"] = hm
            in_maps.append(m)
    return in_maps


_NC_CACHE = {}


def kernel(**inputs):
    cfg = Cfg()
    if "nc" not in _NC_CACHE:
        _NC_CACHE["nc"] = build_program(cfg)
    nc = _NC_CACHE["nc"]
    B = inputs["x"].shape[0]
    in_maps = make_in_maps(cfg, inputs, B)
    res = run_bass_kernel_spmd(nc, in_maps, core_ids=list(range(len(in_maps))))
    own = cfg.NT * cfg.T
    outp = np.empty((B, 2 * own, cfg.D), np.float32)
    i = 0
    for b in range(B):
        for hf in range(2):
            outp[b, hf * own:(hf + 1) * own, :] = res.results[i]["out"]
            i += 1
    return outp
```

```python
import numpy as np
from contextlib import ExitStack
import concourse.bass as bass
import concourse.mybir as mybir
from concourse.bass_utils import run_bass_kernel_spmd

F32 = mybir.dt.float32
BF16 = mybir.dt.bfloat16
AF = mybir.ActivationFunctionType
ALU = mybir.AluOpType

EPS = 1e-6
NEG = -30000.0
POOL_WINDOWS = (2, 4, 8, 16)
PW = 143


class Cfg:
    def __init__(s, D=1024, E=2048, NT=8, T=512, WB=3):
        s.D, s.E, s.NT, s.T, s.WB = D, E, NT, T, WB
        s.KD = D // 128
        s.KE = E // 128
        s.NH = E // 128
        s.GW = E // 4
        s.CPG = s.GW // 128
        s.NB = T // 128
        s.NTOK = 128 + T + NT * T
        s.ZW = min(512, E)
        s.OW = min(256, D)
        r = 0
        s.r_adab = r; r += 4 * 3 * s.KD
        s.r_kvb = r; r += 2 * s.KD
        s.r_ng = r; r += 4 * s.KD
        s.r_kvg = r; r += s.KD
        s.r_fg = r; r += s.KD
        s.r_asc = r; r += 2 * s.KE
        s.r_c = r; r += s.KD
        s.NV = r
        s.NM = 4 * 3 * s.KD + 2 * s.KD


class Op:
    __slots__ = ("eng", "fn", "idx", "deps", "dma_slot", "dma_val", "seq",
                 "sig", "sigval", "waits", "vc")


class Prog:
    ENG = ("pe", "act", "dve", "pool", "sp")

    def __init__(self):
        self.ops = []
        self.lastw = {}
        self.readers = {}
        self.dma_cnt = {}
        self.group_slots = set()

    def add(self, eng, fn, reads=(), writes=(), dma_slot=None):
        o = Op()
        o.eng, o.fn, o.idx, o.dma_slot = eng, fn, len(self.ops), dma_slot
        o.sig = False
        o.dma_val = 0
        deps = {}
        for k in reads:
            w = self.lastw.get(k)
            if w is not None:
                deps[w.idx] = w
        for k in writes:
            w = self.lastw.get(k)
            if w is not None:
                deps[w.idx] = w
            for r in self.readers.get(k, ()):
                deps[r.idx] = r
        for k in reads:
            self.readers.setdefault(k, []).append(o)
        for k in writes:
            self.lastw[k] = o
            self.readers[k] = []
        o.deps = list(deps.values())
        if dma_slot is not None:
            c = self.dma_cnt.get(dma_slot, 0) + 1
            self.dma_cnt[dma_slot] = c
            o.dma_val = 16 * c
        self.ops.append(o)
        return o

    def finalize(self):
        seq = {e: 0 for e in self.ENG}
        for o in self.ops:
            if o.dma_slot is None:
                seq[o.eng] += 1
                o.seq = seq[o.eng]
        K = {e: {} for e in self.ENG}
        for o in self.ops:
            Ke = K[o.eng]
            cand = []
            for d in o.deps:
                if d.dma_slot is not None:
                    key = ("dma", d.dma_slot)
                    val = d.dma_val
                    if d.dma_slot in self.group_slots:
                        val = 16 * self.dma_cnt[d.dma_slot]
                else:
                    if d.eng == "pe" and o.eng == "pe" and o.dma_slot is None:
                        continue
                    key = ("eng", d.eng)
                    val = d.seq
                cand.append((d.idx, key, val, d))
            cand.sort(key=lambda t: -t[0])
            waits = []
            for _, key, val, d in cand:
                if Ke.get(key, 0) >= val:
                    continue
                if key[0] == "eng":
                    d.sig = True
                waits.append((key, val, d))
                for k2, v2 in d.vc.items():
                    if Ke.get(k2, 0) < v2:
                        Ke[k2] = v2
                if Ke.get(key, 0) < val:
                    Ke[key] = val
            o.waits = waits
            vc = dict(Ke)
            if o.dma_slot is None:
                vc[("eng", o.eng)] = o.seq
            else:
                key = ("dma", o.dma_slot)
                val = o.dma_val
                if o.dma_slot in self.group_slots:
                    val = 16 * self.dma_cnt[o.dma_slot]
                if vc.get(key, 0) < val:
                    vc[key] = val
            o.vc = vc
        cnt = {e: 0 for e in self.ENG}
        for o in self.ops:
            if o.dma_slot is None and o.sig:
                cnt[o.eng] += 1
                o.sigval = cnt[o.eng]
        self.sig_counts = cnt

    def emit(self, nc, es):
        self.finalize()
        sems = {}
        for e in ("pe", "act", "dve", "pool"):
            sems[("eng", e)] = es.enter_context(nc.semaphore("s_" + e))
        for slot in self.dma_cnt:
            sems[("dma", slot)] = es.enter_context(nc.semaphore("d_" + str(slot)))
        block = es.enter_context(nc.Block())
        per = {e: [o for o in self.ops if o.eng == e] for e in self.ENG}

        def run(eng_name, eng):
            for o in per[eng_name]:
                ws_ = [(sems[key], d.sigval if key[0] == "eng" else val)
                       for key, val, d in o.waits]
                emb = None
                if ws_ and o.dma_slot is None:
                    emb = ws_.pop()
                for sm, v in ws_:
                    eng.wait_ge(sm, v)
                inst = o.fn(eng)
                if emb is not None:
                    inst._wait_ge(emb[0], emb[1])
                if o.dma_slot is not None:
                    inst.then_inc(sems[("dma", o.dma_slot)], 16)
                elif o.sig:
                    inst.then_inc(sems[("eng", o.eng)], 1)

        @block.tensor
        def _(e):
            run("pe", e)

        @block.scalar
        def _(e):
            run("act", e)

        @block.vector
        def _(e):
            run("dve", e)

        @block.gpsimd
        def _(e):
            run("pool", e)

        @block.sync
        def _(e):
            run("sp", e)


def weight_schedule(cfg):
    KD, KE = cfg.KD, cfg.KE
    S = []

    def ada(l):
        cols = 3 * cfg.D
        pw = 512 if cols % 512 == 0 else 256
        for c in range(cols // pw):
            S.append(("ada", l, c, pw))

    def kvada():
        cols = 2 * cfg.D
        pw = 512 if cols % 512 == 0 else 256
        for c in range(cols // pw):
            S.append(("kvada", 0, c, pw))

    for l in range(4):
        ada(l)
    kvada()

    def alayer(a, only_val):
        if not only_val:
            for c in range(cfg.E // cfg.ZW):
                S.append(("a_z", a, c, cfg.ZW))
        for g in range(4):
            S.append(("a_val", a, g, cfg.GW))
            if not only_val:
                S.append(("a_wg", a, g, cfg.GW))
        if not only_val:
            for c in range(cfg.D // cfg.OW):
                S.append(("a_wo", a, c, cfg.OW))

    def kv():
        for c in range(cfg.E // cfg.ZW):
            S.append(("kv_k", 0, c, cfg.ZW))
        for c in range(cfg.E // cfg.ZW):
            S.append(("kv_v", 0, c, cfg.ZW))

    def blayer(b):
        npc = cfg.E // cfg.ZW
        for c in range(npc):
            S.append(("b_z", b, c, cfg.ZW))
        for c in range(npc):
            S.append(("b_q", b, c, cfg.ZW))
        for c in range(cfg.D // cfg.OW):
            S.append(("b_wo", b, c, cfg.OW))

    alayer(0, False)
    alayer(1, True)
    alayer(0, False)
    alayer(1, False)
    kv()
    for _ in range(cfg.NT):
        alayer(0, False)
        alayer(1, False)
        kv()
        blayer(0)
        blayer(1)
    return S


def build_program(cfg):
    D, E, KD, KE, NH, GW, CPG, T, NT = (cfg.D, cfg.E, cfg.KD, cfg.KE, cfg.NH,
                                        cfg.GW, cfg.CPG, cfg.T, cfg.NT)
    nc = bass.Bass("TRN2", target_bir_lowering=False)
    es = ExitStack()

    def dram(name, shape, kind="ExternalInput"):
        return nc.dram_tensor(name, list(shape), F32, kind=kind).ap()

    xin = dram("xin", [cfg.NTOK, D])
    vecs = dram("vecs", [cfg.NV, 128])
    ada_w = dram("ada_w", [4, D, 3 * D])
    kv_ada_w = dram("kv_ada_w", [D, 2 * D])
    a_w_in = dram("a_w_in", [2, D, 2 * E])
    a_w_group = dram("a_w_group", [2, 4, GW, GW])
    a_w_out = dram("a_w_out", [2, E, D])
    w_kv = dram("w_kv", [D, 2 * E])
    b_w_in = dram("b_w_in", [2, D, 2 * E])
    b_w_out = dram("b_w_out", [2, E, D])
    biasn = dram("biasn", [2, NH, 128, 256])
    cfar = dram("cfar", [128, 2 * NH])
    hmask = dram("hmask", [128, 1])
    pm_std = dram("pm_std", [128, 4, PW])
    pm0 = dram("pm0", [128, 4, PW + 15])
    identm = dram("identm", [128, 128])
    out = dram("out", [NT * T, D], kind="ExternalOutput")
    escr = nc.dram_tensor("escr", [2, 128, NH * 256], BF16, kind="Internal").ap()

    def sb(name, shape, dt):
        return es.enter_context(nc.sbuf_tensor(name, list(shape), dt))

    def psb(name):
        return es.enter_context(nc.psum_tensor(name, [128, 512], F32))

    h = sb("h", [128, KD, T], F32)
    u = sb("u", [128, KD, T], BF16)
    kring = sb("kring", [128, NH, 2 * T], BF16)
    vring = sb("vring", [128, 2 * cfg.NB, E], BF16)
    hist = sb("hist", [128, 2, E], BF16)
    sz = sb("sz", [128, KE, T], BF16)
    bufq = sb("bufq", [128, KE * T], BF16)
    wbuf = [sb(f"wbuf{i}", [128, 4096], BF16) for i in range(cfg.WB)]
    enear = sb("enear", [128, NH, 256], BF16)
    xst = [sb(f"xst{i}", [128, D], F32) for i in range(2)]
    ost = [sb(f"ost{i}", [128, 512], F32) for i in range(4)]
    NSQ = 4
    sq = [sb(f"sq{i}", [128, T], BF16) for i in range(NSQ)]
    rt = sb("rt", [128, T], F32)
    rstd = sb("rstd", [128, T], F32)
    tmpf = [sb(f"tmpf{i}", [128, T], F32) for i in range(2)]
    rd = [sb(f"rd{i}", [128, T], F32) for i in range(2)]
    NPT = 6
    LOOK = 3
    pT = [sb(f"pT{i}", [128, 512], BF16) for i in range(NPT)]
    vst = sb("vst", [128, 128], F32)
    vecT = sb("vecT", [128, cfg.NV], F32)
    modT = sb("modT", [128, cfg.NM], F32)
    gm = sb("gm", [128, 5 * KD], F32)
    cact = sb("cact", [128, KD], BF16)
    cfar_t = sb("cfar_t", [128, 2 * NH], F32)
    cfsc = sb("cfsc", [128, 2 * NH], F32)
    cneg = sb("cneg", [128, 2 * NH], F32)
    chalo = sb("chalo", [128, 2 * NH], F32)
    hm_t = sb("hm_t", [128, 1], F32)
    pms = sb("pms", [128, 4, PW], BF16)
    pm0s = sb("pm0s", [128, 4, PW + 15], BF16)
    ident = sb("ident", [128, 128], F32)
    ones = sb("ones", [128, 128], BF16)
    epsb = sb("epsb", [128, 1], F32)
    fcell = sb("fcell", [128, 2], F32)

    banks = [psb(f"ps{i}") for i in range(8)]

    P = Prog()
    P.group_slots.add("const_sp")
    P.group_slots.add("const_pool")

    rr = {"a": 0, "s": 0, "evac": 0, "pt": 0, "xst": 0, "ost": 0, "bst": 0, "sq": 0,
          "tmpf": 0}

    def nextbank():
        b = rr["a"] % 4
        rr["a"] += 1
        return b

    def mm(bank, out_ap, lhsT, rhs, start, reads, skip=False):
        def f(e, out_ap=out_ap, lhsT=lhsT, rhs=rhs, start=start, skip=skip):
            return e.matmul(out_ap, lhsT, rhs, start=start, stop=True,
                            skip_group_check=True)
        return P.add("pe", f, reads=reads, writes=[("ps", bank)])

    def act(out_ap, in_ap, func, reads, writes, bias=None, scale=None):
        def f(e):
            kw = {}
            if bias is not None:
                kw["bias"] = bias
            if scale is not None:
                kw["scale"] = scale
            return e.activation(out=out_ap, in_=in_ap, func=func, **kw)
        return P.add("act", f, reads=reads, writes=writes)

    def dve(fn, reads, writes):
        return P.add("dve", fn, reads=reads, writes=writes)

    def evac(out_ap, in_ap, reads, writes, prefer=None):
        rr["evac"] += 1
        if prefer == "act" or (prefer is None and rr["evac"] % 2 == 0):
            return act(out_ap, in_ap, AF.Copy, reads, writes)
        return dve(lambda e: e.tensor_copy(out=out_ap, in_=in_ap), reads, writes)

    def dma(queue, out_ap, in_ap, slot, reads, writes):
        return P.add(queue, lambda e: e.dma_start(out=out_ap, in_=in_ap),
                     reads=reads, writes=writes, dma_slot=slot)

    sched = weight_schedule(cfg)
    ws = {"next_load": 0, "next_use": 0}

    def piece_src(desc):
        kind, l, c, pw = desc
        if kind == "ada":
            src, kc = ada_w[l, :, c * pw:(c + 1) * pw], KD
        elif kind == "kvada":
            src, kc = kv_ada_w[:, c * pw:(c + 1) * pw], KD
        elif kind == "a_z":
            src, kc = a_w_in[l, :, E + c * pw:E + (c + 1) * pw], KD
        elif kind == "a_val":
            src, kc = a_w_in[l, :, c * pw:(c + 1) * pw], KD
        elif kind == "a_wg":
            src, kc = a_w_group[l, c, :, :], CPG
        elif kind == "a_wo":
            src, kc = a_w_out[l, :, c * pw:(c + 1) * pw], KE
        elif kind == "kv_k":
            src, kc = w_kv[:, c * pw:(c + 1) * pw], KD
        elif kind == "kv_v":
            src, kc = w_kv[:, E + c * pw:E + (c + 1) * pw], KD
        elif kind == "b_q":
            src, kc = b_w_in[l, :, c * pw:(c + 1) * pw], KD
        elif kind == "b_z":
            src, kc = b_w_in[l, :, E + c * pw:E + (c + 1) * pw], KD
        elif kind == "b_wo":
            src, kc = b_w_out[l, :, c * pw:(c + 1) * pw], KE
        else:
            raise ValueError(kind)
        return src.rearrange("(kc p) n -> p kc n", p=128), kc, pw

    def issue_load():
        i = ws["next_load"]
        if i >= len(sched):
            return
        ws["next_load"] += 1
        slot = i % cfg.WB
        src, kc, pw = piece_src(sched[i])
        dst = wbuf[slot][:, 0:kc * pw].rearrange("p (kc n) -> p kc n", kc=kc)
        dma("pool", dst, src, f"w{slot}", reads=[], writes=[("wbuf", slot)])

    def get_piece(kind, l, c):
        i = ws["next_use"]
        assert sched[i][:3] == (kind, l, c), (sched[i], kind, l, c)
        ws["next_use"] += 1
        slot = i % cfg.WB
        _, kc, pw = piece_src(sched[i])
        view = wbuf[slot][:, 0:kc * pw].rearrange("p (kc n) -> p kc n", kc=kc)
        return view, ("wbuf", slot)

    def release_piece():
        issue_load()

    P.add("dve", lambda e: e.memset(ones[:], 1.0), writes=[("ones",)])
    P.add("dve", lambda e: e.memset(epsb[:], EPS), writes=[("epsb",)])
    P.add("dve", lambda e: e.memset(hist[:].rearrange("p a e -> p (a e)"), 0.0),
          writes=[("hist", 0), ("hist", 1)])
    dma("sp", ident[:], identm[:, :], "const_sp", [], [("ident",)])
    dma("sp", cfar_t[:], cfar[:, :], "const_sp", [], [("cfar",)])
    dma("sp", hm_t[:], hmask[:, :], "const_sp", [], [("hm",)])
    dma("pool", pms[:], pm_std[:, :, :], "const_pool", [], [("pms",)])
    dma("pool", pm0s[:], pm0[:, :, :], "const_pool", [], [("pm0s",)])
    for i in range(cfg.WB):
        issue_load()

    r0 = 0
    while r0 < cfg.NV:
        n = min(128, cfg.NV - r0)
        dma("sp", vst[0:n, :], vecs[r0:r0 + n, :], "vst", [], [("vst",)])
        b = nextbank()
        P.add("pe", lambda e, n=n, b=b: e.transpose(
            banks[b][:, 0:n], vst[0:n, :], ident[0:n, 0:n]),
            reads=[("vst",), ("ident",)], writes=[("ps", b)])
        dve(lambda e, n=n, b=b, r0=r0: e.tensor_copy(
            out=vecT[:, r0:r0 + n], in_=banks[b][:, 0:n]),
            reads=[("ps", b)], writes=[("vecT",)])
        r0 += n

    act(cact[:], vecT[:, cfg.r_c:cfg.r_c + KD], AF.Silu, [("vecT",)], [("cact",)])
    dve(lambda e: e.tensor_copy(out=cfsc[:], in_=cfar_t[:]), [("cfar",)], [("cfsc",)])
    dve(lambda e: e.tensor_scalar(out=cneg[:], in0=cfar_t[:], scalar1=-1.0,
                                  scalar2=None, op0=ALU.mult),
        [("cfar",)], [("cneg",)])
    dve(lambda e: e.tensor_scalar(out=chalo[:], in0=cfar_t[:], scalar1=hm_t[:, 0:1],
                                  scalar2=None, op0=ALU.add),
        [("cfar",), ("hm",)], [("chalo",)])

    h_flat = h[:].rearrange("p k t -> p (k t)")
    sz_flat = sz[:].rearrange("p k t -> p (k t)")
    assert KD * T == NH * 256 and KE * T == 2 * NH * 256
    hkeys = [("h", k) for k in range(KD)]
    szkeys = [("sz", k) for k in range(KE)]
    for bi_ in range(2):
        dma("sp", h_flat.rearrange("p (h c) -> p h c", h=NH),
            biasn[bi_, :, :, :].rearrange("h p c -> p h c"), "bin", [], hkeys)
        for hh in range(NH):
            o0 = bi_ * NH * 256 + hh * 256
            act(sz_flat[:, o0:o0 + 256], h_flat[:, hh * 256:(hh + 1) * 256], AF.Exp,
                hkeys + [("cneg",)], szkeys,
                bias=cneg[:, bi_ * NH + hh:bi_ * NH + hh + 1], scale=1.0)
        dma("sp", escr[bi_, :, :], sz_flat[:, bi_ * NH * 256:(bi_ + 1) * NH * 256], "bout",
            szkeys, [("escr", bi_)])

    def load_enear(bi_):
        dma("sp", enear[:].rearrange("p h c -> p (h c)"), escr[bi_, :, :], "en",
            [("escr", bi_)], [("enear", hh) for hh in range(NH)])

    mb = 7
    first = True
    for desc in [d for d in sched if d[0] in ("ada", "kvada")]:
        kind, l, c, pw = desc
        wv, wkey = get_piece(kind, l, c)
        base = (l * 3 * KD if kind == "ada" else 4 * 3 * KD) + c * (pw // 128)
        for j in range(pw // 128):
            col = base + j
            for kc in range(KD):
                mm(mb, banks[mb][:, col:col + 1], wv[:, kc, j * 128:(j + 1) * 128],
                   cact[:, kc:kc + 1], start=(kc == 0),
                   reads=[wkey, ("cact",)], skip=True)
        release_piece()
    dve(lambda e: e.tensor_tensor(out=modT[:], in0=banks[mb][:, 0:cfg.NM],
                                  in1=vecT[:, cfg.r_adab:cfg.r_adab + cfg.NM],
                                  op=ALU.add),
        [("ps", mb), ("vecT",)], [("modT",)])
    for l in range(4):
        sc0 = l * 3 * KD + KD
        dve(lambda e, l=l, sc0=sc0: e.scalar_tensor_tensor(
            out=gm[:, l * KD:(l + 1) * KD], in0=modT[:, sc0:sc0 + KD], scalar=1.0,
            in1=vecT[:, cfg.r_ng + l * KD:cfg.r_ng + (l + 1) * KD],
            op0=ALU.add, op1=ALU.mult), [("modT",), ("vecT",)], [("gm",)])
    kvb = 4 * 3 * KD
    dve(lambda e: e.scalar_tensor_tensor(
        out=gm[:, 4 * KD:5 * KD], in0=modT[:, kvb + KD:kvb + 2 * KD], scalar=1.0,
        in1=vecT[:, cfg.r_kvg:cfg.r_kvg + KD], op0=ALU.add, op1=ALU.mult),
        [("modT",), ("vecT",)], [("gm",)])

    def shift_col(l, k):
        return modT[:, l * 3 * KD + k:l * 3 * KD + k + 1]

    def gate_col(l, k):
        return modT[:, l * 3 * KD + 2 * KD + k:l * 3 * KD + 2 * KD + k + 1]

    xq = {"issued": {}}

    def issue_x(tok0, tb):
        if (tok0, tb) in xq["issued"]:
            return
        s = rr["xst"] % 2
        rr["xst"] += 1
        dma("sp", xst[s][:], xin[tok0 + tb * 128:tok0 + (tb + 1) * 128, :],
            f"x{s}", [], [("xst", s)])
        xq["issued"][(tok0, tb)] = s

    def load_x(tok0, nb, Tt, nxt=None):
        for tb in range(nb):
            issue_x(tok0, tb)
            if tb + 1 < nb:
                issue_x(tok0, tb + 1)
            s = xq["issued"][(tok0, tb)]
            for k0 in range(0, KD, 4):
                nk = min(4, KD - k0)
                b = nextbank()
                for kk in range(nk):
                    k = k0 + kk
                    P.add("pe", lambda e, b=b, kk=kk, k=k, s=s: e.transpose(
                        banks[b][:, kk * 128:(kk + 1) * 128],
                        xst[s][:, k * 128:(k + 1) * 128], ident[:]),
                        reads=[("xst", s), ("ident",)], writes=[("ps", b)])
                evac(h[:, k0:k0 + nk, tb * 128:(tb + 1) * 128],
                     banks[b][:, 0:nk * 128].rearrange("p (k t) -> p k t", k=nk),
                     [("ps", b)], [("h", k) for k in range(k0, k0 + nk)])
        if nxt is not None:
            issue_x(nxt[0], 0)
            if nxt[1] > 1:
                issue_x(nxt[0], 1)

    slru = [3, 0, 1, 2]
    SB_ = 7
    st = {"ready": False, "nmm": 0}

    def stats_square(k, Tt):
        s_ = rr["sq"] % NSQ
        rr["sq"] += 1
        act(sq[s_][:, 0:Tt], h[:, k, 0:Tt], AF.Square, [("h", k)], [("sq", s_)])
        return s_

    def stats_mm(s_, Tt):
        mm(SB_, banks[SB_][:, 0:Tt], ones[:], sq[s_][:, 0:Tt], start=(st["nmm"] == 0),
           reads=[("ones",), ("sq", s_)])
        st["nmm"] += 1

    def stats_finish(Tt):
        assert st["nmm"] == KD
        st["nmm"] = 0
        act(rt[:, 0:Tt], banks[SB_][:, 0:Tt], AF.Ln, [("ps", SB_), ("epsb",)], [("rt",)],
            bias=epsb[:, 0:1], scale=1.0 / D)
        act(rstd[:, 0:Tt], rt[:, 0:Tt], AF.Exp, [("rt",)], [("rstd",)], scale=-0.5)
        st["ready"] = True

    def rms_stats(Tt):
        if st["ready"]:
            st["ready"] = False
            return
        pend = []
        for k in range(KD):
            pend.append(stats_square(k, Tt))
            if len(pend) > 1:
                stats_mm(pend.pop(0), Tt)
        for s_ in pend:
            stats_mm(s_, Tt)
        stats_finish(Tt)
        st["ready"] = False

    def modulate(Tt, gcol0, shift_fn):
        for k in range(KD):
            s = rr["tmpf"] % 2
            rr["tmpf"] += 1
            dve(lambda e, k=k, s=s: e.tensor_tensor(
                out=tmpf[s][:, 0:Tt], in0=h[:, k, 0:Tt], in1=rstd[:, 0:Tt], op=ALU.mult),
                [("h", k), ("rstd",)], [("tmpf", s)])
            act(u[:, k, 0:Tt], tmpf[s][:, 0:Tt], AF.Identity,
                [("tmpf", s), ("gm",), ("modT",)], [("u", k)],
                bias=shift_fn(k), scale=gm[:, gcol0 + k:gcol0 + k + 1])

    def proj_fm(kind, l, width, Tt, consume):
        npieces = E // width
        for c in range(npieces):
            wv, wkey = get_piece(kind, l, c)
            nj = width // 128
            if c == 0 and nj <= 4:
                bs = [nextbank() for _ in range(nj)]
                for kc in range(KD):
                    for j in range(nj):
                        mm(bs[j], banks[bs[j]][:, 0:Tt], wv[:, kc, j * 128:(j + 1) * 128],
                           u[:, kc, 0:Tt], start=(kc == 0), reads=[wkey, ("u", kc)])
                for j in range(nj):
                    consume(c * nj + j, bs[j])
            else:
                for j in range(nj):
                    oc = c * nj + j
                    b = nextbank()
                    for kc in range(KD):
                        mm(b, banks[b][:, 0:Tt], wv[:, kc, j * 128:(j + 1) * 128],
                           u[:, kc, 0:Tt], start=(kc == 0),
                           reads=[wkey, ("u", kc)])
                    consume(oc, b)
            release_piece()

    def out_proj(kind, l, layer, Tt):
        pend = []
        for c in range(D // cfg.OW):
            wv, wkey = get_piece(kind, l, c)
            for j in range(cfg.OW // 128):
                dc = c * (cfg.OW // 128) + j
                b = nextbank()
                for ec in range(KE):
                    mm(b, banks[b][:, 0:Tt], wv[:, ec, j * 128:(j + 1) * 128],
                       sz[:, ec, 0:Tt], start=(ec == 0),
                       reads=[wkey, ("sz", ec)])
                dve(lambda e, b=b, dc=dc: e.scalar_tensor_tensor(
                    out=h[:, dc, 0:Tt], in0=banks[b][:, 0:Tt], scalar=gate_col(layer, dc),
                    in1=h[:, dc, 0:Tt], op0=ALU.mult, op1=ALU.add),
                    [("ps", b), ("modT",), ("h", dc)], [("h", dc)])
                pend.append(stats_square(dc, Tt))
                if len(pend) > 2:
                    stats_mm(pend.pop(0), Tt)
            release_piece()
        for s_ in pend:
            stats_mm(s_, Tt)
        stats_finish(Tt)

    val_v = [bufq[:, i * cfg.NB * GW:(i + 1) * cfg.NB * GW].rearrange(
        "p (b n) -> p b n", b=cfg.NB) for i in range(2)]
    off = 2 * cfg.NB * GW
    pooled_v = [bufq[:, off + i * CPG * T: off + (i + 1) * CPG * T].rearrange(
        "p (c t) -> p c t", c=CPG) for i in range(2)]
    q_v = bufq[:, 0:KE * T].rearrange("p (h t) -> p h t", h=KE)

    BQ = ("bufq_mode",)

    def fence():
        dve(lambda e: e.memset(fcell[:, 0:1], 0.0), [], [BQ])

    def a_layer(a, Tt, only_val, first_real):
        layer = a
        nb = Tt // 128
        if a == 0:
            fence()
        rms_stats(Tt)
        modulate(Tt, layer * KD, lambda k: shift_col(layer, k))
        if not only_val:
            def z_consume(oc, b):
                act(sz[:, oc, 0:Tt], banks[b][:, 0:Tt], AF.Silu, [("ps", b)], [("sz", oc)])
            proj_fm("a_z", a, cfg.ZW, Tt, z_consume)
        for g in range(4):
            vs = g % 2
            wv, wkey = get_piece("a_val", a, g)
            for tb in range(nb):
                b = nextbank()
                for kc in range(KD):
                    mm(b, banks[b][:, 0:GW], u[:, kc, tb * 128:(tb + 1) * 128],
                       wv[:, kc, 0:GW], start=(kc == 0), reads=[wkey, ("u", kc)])
                evac(val_v[vs][:, tb, 0:GW], banks[b][:, 0:GW], [("ps", b), BQ],
                     [("val", vs, tb)])
            release_piece()
            if not only_val:
                wg, wgkey = get_piece("a_wg", a, g)
                for ci in range(CPG):
                    ec = g * CPG + ci
                    b = nextbank()
                    if first_real:
                        ph = pm0s[:, g, PW:PW + 15]
                    else:
                        ph = pms[:, g, 128:PW]
                    mm(b, banks[b][:, 0:15], hist[:, a, ec * 128:(ec + 1) * 128], ph,
                       start=True, reads=[("hist", a), ("pms",), ("pm0s",)], skip=True)
                    for sbk in range(nb):
                        ncols = min(PW, Tt - 128 * sbk)
                        if first_real and sbk == 0:
                            pr = pm0s[:, g, 0:ncols]
                        else:
                            pr = pms[:, g, 0:ncols]
                        mm(b, banks[b][:, 128 * sbk:128 * sbk + ncols],
                           val_v[vs][:, sbk, ci * 128:(ci + 1) * 128], pr, start=False,
                           reads=[("val", vs, sbk), ("pms",), ("pm0s",), BQ], skip=True)
                    evac(pooled_v[vs][:, ci, 0:Tt], banks[b][:, 0:Tt], [("ps", b), BQ],
                         [("pooled", vs, ci)])
            dve(lambda e, vs=vs, g=g: e.tensor_copy(
                out=hist[:, a, g * GW:(g + 1) * GW], in_=val_v[vs][:, nb - 1, 0:GW]),
                [("val", vs, nb - 1), BQ], [("hist", a)])
            if not only_val:
                for co in range(CPG):
                    oc = g * CPG + co
                    b = nextbank()
                    for ci in range(CPG):
                        mm(b, banks[b][:, 0:Tt], wg[:, ci, co * 128:(co + 1) * 128],
                           pooled_v[vs][:, ci, 0:Tt], start=(ci == 0),
                           reads=[wgkey, ("pooled", vs, ci), BQ])
                    asc = vecT[:, cfg.r_asc + a * KE + oc:cfg.r_asc + a * KE + oc + 1]
                    dve(lambda e, b=b, oc=oc, asc=asc: e.scalar_tensor_tensor(
                        out=sz[:, oc, 0:Tt], in0=banks[b][:, 0:Tt], scalar=asc,
                        in1=sz[:, oc, 0:Tt], op0=ALU.mult, op1=ALU.mult),
                        [("ps", b), ("vecT",), ("sz", oc)], [("sz", oc)])
                release_piece()
        if not only_val:
            out_proj("a_wo", a, layer, Tt)

    def kv_layer(slot):
        Tt = T
        rms_stats(Tt)
        kvb_ = 4 * 3 * KD
        modulate(Tt, 4 * KD, lambda k: modT[:, kvb_ + k:kvb_ + k + 1])

        def k_consume(oc, b):
            evac(kring[:, oc, slot * T:(slot + 1) * T], banks[b][:, 0:Tt], [("ps", b)],
                 [("kring", slot, oc)])
        proj_fm("kv_k", 0, cfg.ZW, Tt, k_consume)
        for c in range(E // cfg.ZW):
            wv, wkey = get_piece("kv_v", 0, c)
            for tb in range(cfg.NB):
                b = nextbank()
                for kc in range(KD):
                    mm(b, banks[b][:, 0:cfg.ZW], u[:, kc, tb * 128:(tb + 1) * 128],
                       wv[:, kc, 0:cfg.ZW], start=(kc == 0), reads=[wkey, ("u", kc)])
                evac(vring[:, slot * cfg.NB + tb, c * cfg.ZW:(c + 1) * cfg.ZW],
                     banks[b][:, 0:cfg.ZW], [("ps", b)], [("vring", slot * cfg.NB + tb, c)])
            release_piece()

    sm_scale = float(128 ** -0.5)

    def b_layer(bi, slot, first_real, reuse_stats):
        layer = 2 + bi
        Tt = T
        if bi == 0:
            fence()
        if not reuse_stats:
            rms_stats(Tt)
        modulate(Tt, layer * KD, lambda k: shift_col(layer, k))

        if first_real:
            groups = [[3], [4], [0, 2], [1], [6], [7, 5]]
        else:
            groups = [[3], [4], [0, 2], [1, 6], [7, 5]]
        NG = len(groups)
        steps = [(hh, gi) for hh in range(NH) for gi in range(NG)]
        info = {}

        def blk(j):
            a_lo = max(0, 2 * j - 8)
            a_hi = min(7, 2 * j + 1)
            return a_lo, a_hi, 64 * a_lo, 64 * (a_hi + 1)

        def stage1(idx):
            hh, gi = steps[idx]
            lh = bi * NH + hh
            allowed = (0, 1, 2, 3) if fpos[0] >= len(fifo) else (0, 1, 2)
            sbk = next(b_ for b_ in slru if b_ in allowed)
            slru.remove(sbk)
            slru.append(sbk)
            ps_ = rr["pt"] % NPT
            rr["pt"] += 1
            off = 0
            members = []
            for n_, j in enumerate(groups[gi]):
                a_lo, a_hi, qlo, qhi = blk(j)
                ncols = qhi - qlo
                if j < 4:
                    kcol = (1 - slot) * T + j * 128
                    vblk = (1 - slot) * cfg.NB + j
                    kkey = ("kring", 1 - slot, hh)
                else:
                    kcol = slot * T + (j - 4) * 128
                    vblk = slot * cfg.NB + (j - 4)
                    kkey = ("kring", slot, hh)
                mm(sbk, banks[sbk][:, off:off + ncols], kring[:, hh, kcol:kcol + 128],
                   q_v[:, hh, qlo:qhi], start=(n_ == 0), reads=[kkey, ("q", hh), BQ])
                members.append((j, off, ncols, qlo, qhi, vblk, a_lo, a_hi))
                off += ncols
            total = off
            assert total <= 512
            if first_real and groups[gi][0] < 4:
                bcol = chalo[:, lh:lh + 1]
            else:
                bcol = cfsc[:, lh:lh + 1]
            act(pT[ps_][:, 0:total], banks[sbk][:, 0:total], AF.Exp,
                [("ps", sbk), ("cfsc",), ("chalo",)], [("pT", ps_)],
                bias=bcol, scale=sm_scale)
            for (j, off, ncols, qlo, qhi, vblk, a_lo, a_hi) in members:
                n0 = max(a_lo, 2 * j - 8)
                n1 = min(a_hi, 2 * j - 5)
                if n1 >= n0:
                    c0 = off + (n0 - a_lo) * 64
                    c1 = off + (n1 + 1 - a_lo) * 64
                    e0 = (n0 - (2 * j - 8)) * 64
                    e1 = (n1 + 1 - (2 * j - 8)) * 64
                    dve(lambda e, ps_=ps_, c0=c0, c1=c1, e0=e0, e1=e1, hh=hh: e.tensor_tensor(
                        out=pT[ps_][:, c0:c1], in0=pT[ps_][:, c0:c1],
                        in1=enear[:, hh, e0:e1], op=ALU.mult),
                        [("pT", ps_), ("enear", hh)], [("pT", ps_)])
                if 2 * j + 1 <= 7:
                    m0 = off + (2 * j + 1 - a_lo) * 64
                    dve(lambda e, ps_=ps_, m0=m0: e.memset(pT[ps_][0:64, m0:m0 + 64], 0.0),
                        [], [("pT", ps_)])
            info[idx] = (ps_, members)

        def stage2(idx):
            hh, gi = steps[idx]
            ps_, members = info.pop(idx)
            ob = 4 + (hh % 2)
            db = 6 + (hh % 2)
            for n_, (j, off, ncols, qlo, qhi, vblk, a_lo, a_hi) in enumerate(members):
                first = (gi == 0 and n_ == 0)
                vkeys = [("vring", vblk, c) for c in range(E // cfg.ZW)]
                mm(ob, banks[ob][:, qlo:qhi], vring[:, vblk, hh * 128:(hh + 1) * 128],
                   pT[ps_][:, off:off + ncols], start=first, reads=vkeys + [("pT", ps_)])
            for n_, (j, off, ncols, qlo, qhi, vblk, a_lo, a_hi) in enumerate(members):
                first = (gi == 0 and n_ == 0)
                mm(db, banks[db][:, qlo:qhi], ones[:], pT[ps_][:, off:off + ncols],
                   start=first, reads=[("ones",), ("pT", ps_)])
            if gi == NG - 1:
                r = hh % 2
                act(rd[r][:, 0:Tt], banks[db][:, 0:Tt], AF.Ln, [("ps", db)], [("rd", r)])
                act(rd[r][:, 0:Tt], rd[r][:, 0:Tt], AF.Exp, [("rd", r)], [("rd", r)],
                    scale=-1.0)
                dve(lambda e, r=r, hh=hh: e.tensor_tensor(
                    out=rd[r][:, 0:Tt], in0=rd[r][:, 0:Tt], in1=sz[:, hh, 0:Tt], op=ALU.mult),
                    [("rd", r), ("sz", hh)], [("rd", r)])
                dve(lambda e, r=r, hh=hh, ob=ob: e.tensor_tensor(
                    out=sz[:, hh, 0:Tt], in0=banks[ob][:, 0:Tt], in1=rd[r][:, 0:Tt],
                    op=ALU.mult), [("ps", ob), ("rd", r)], [("sz", hh)])

        HPP = cfg.ZW // 128
        npc = E // cfg.ZW
        pieces = {}
        PB = 3

        def proj_items(kind, hh, rotate=False):
            c, jj = divmod(hh, HPP)
            items = []
            bk = [PB]
            for kc in range(KD):
                def f(kc=kc):
                    if (kind, c) not in pieces:
                        pieces[(kind, c)] = get_piece(kind, bi, c)
                    wv, wkey = pieces[(kind, c)]
                    if kc == 0 and rotate:
                        bk[0] = nextbank()
                    pb = bk[0]
                    mm(pb, banks[pb][:, 0:Tt], wv[:, kc, jj * 128:(jj + 1) * 128],
                       u[:, kc, 0:Tt], start=(kc == 0), reads=[wkey, ("u", kc)])
                    if kc == KD - 1:
                        if kind == "b_q":
                            evac(q_v[:, hh, 0:Tt], banks[pb][:, 0:Tt], [("ps", pb), BQ],
                                 [("q", hh)], prefer="dve")
                        else:
                            act(sz[:, hh, 0:Tt], banks[pb][:, 0:Tt], AF.Silu,
                                [("ps", pb)], [("sz", hh)])
                        if jj == HPP - 1 or hh == NH - 1:
                            release_piece()
                items.append(f)
            return items

        nfirst = min(HPP, NH, 4)
        first_items = [proj_items("b_z", hh, rotate=True) for hh in range(nfirst)]
        for kc in range(KD):
            for hh in range(nfirst):
                first_items[hh][kc]()
        for hh in range(nfirst, NH):
            for f in proj_items("b_z", hh, rotate=True):
                f()
        for hh in range(min(HPP, NH)):
            for f in proj_items("b_q", hh, rotate=True):
                f()
        fifo = []
        for hh in range(HPP, NH):
            fifo.extend(proj_items("b_q", hh))
        fpos = [0]

        def do_work(idx):
            per_step = max(1, KD // 4)
            for _ in range(per_step):
                if fpos[0] < len(fifo):
                    fifo[fpos[0]]()
                    fpos[0] += 1

        def sbank():
            b = rr["s"] % 3
            rr["s"] += 1
            return b

        for idx in range(len(steps) + LOOK):
            if idx < len(steps):
                do_work(idx)
                stage1(idx)
            if idx >= LOOK:
                stage2(idx - LOOK)
        while fpos[0] < len(fifo):
            fifo[fpos[0]]()
            fpos[0] += 1
        if bi == 0:
            load_enear(1)
        out_proj("b_wo", bi, layer, Tt)

    NOST = 4

    def final_out(row0):
        Tt = T
        rms_stats(Tt)
        for k0 in range(0, KD, 4):
            nk = min(4, KD - k0)
            bks = [nextbank() for _ in range(cfg.NB)]
            for kk in range(nk):
                k = k0 + kk
                s_ = rr["tmpf"] % 2
                rr["tmpf"] += 1
                fg = vecT[:, cfg.r_fg + k:cfg.r_fg + k + 1]
                dve(lambda e, k=k, fg=fg, s_=s_: e.scalar_tensor_tensor(
                    out=tmpf[s_][:, 0:Tt], in0=h[:, k, 0:Tt], scalar=fg, in1=rstd[:, 0:Tt],
                    op0=ALU.mult, op1=ALU.mult),
                    [("h", k), ("vecT",), ("rstd",)], [("tmpf", s_)])
                for tb in range(cfg.NB):
                    b = bks[tb]
                    P.add("pe", lambda e, b=b, kk=kk, s_=s_, tb=tb: e.transpose(
                        banks[b][:, kk * 128:(kk + 1) * 128],
                        tmpf[s_][:, tb * 128:(tb + 1) * 128], ident[:]),
                        reads=[("tmpf", s_), ("ident",)], writes=[("ps", b)])
            for tb in range(cfg.NB):
                b = bks[tb]
                so = rr["ost"] % NOST
                rr["ost"] += 1
                evac(ost[so][:, 0:nk * 128], banks[b][:, 0:nk * 128], [("ps", b)],
                     [("ost", so)])
                dma("sp", out[row0 + tb * 128:row0 + (tb + 1) * 128, k0 * 128:(k0 + nk) * 128],
                    ost[so][:, 0:nk * 128], f"o{so}", [("ost", so)], [("outrow", so)])

    load_x(0, 1, 128, nxt=(128, cfg.NB))
    a_layer(0, 128, False, False)
    a_layer(1, 128, True, False)
    load_x(128, cfg.NB, T, nxt=(128 + T, cfg.NB))
    a_layer(0, T, False, False)
    a_layer(1, T, False, False)
    kv_layer(0)
    for it in range(NT):
        slot = (it + 1) % 2
        nxt = (128 + T + (it + 1) * T, cfg.NB) if it + 1 < NT else None
        load_x(128 + T + it * T, cfg.NB, T, nxt=nxt)
        load_enear(0)
        a_layer(0, T, False, it == 0)
        a_layer(1, T, False, it == 0)
        kv_layer(slot)
        b_layer(0, slot, it == 0, True)
        b_layer(1, slot, it == 0, False)
        final_out(it * T)
    assert ws["next_use"] == len(sched), (ws["next_use"], len(sched))

    last = P.add("sp", lambda e: e.nop(), reads=[("outrow", i) for i in range(4)])
    P.emit(nc, es)
    es.close()
    return nc


def pool_matrices():
    pm = np.zeros((128, 4, PW), np.float32)
    pstart = np.zeros((128, 4, PW), np.float32)
    for g, w in enumerate(POOL_WINDOWS):
        for t in range(PW):
            for s in range(max(0, t - w + 1), min(t, 127) + 1):
                pm[s, g, t] += 1.0 / w
                pstart[s, g, t] += 1.0 / min(t + 1, w)
            if t < 128:
                pm[t, g, t] -= 1.0
                pstart[t, g, t] -= 1.0
    return pm, pstart


def near_bias(rel_bias, NH):
    nl = rel_bias.shape[0]
    ki = np.arange(64)[:, None]
    qi = np.arange(64)[None, :]
    outb = np.full((nl, NH, 128, 256), NEG, np.float32)
    for i in range(2):
        for ap in range(4):
            delta = ap - i
            if delta < 0:
                continue
            idx = np.clip(64 * delta + qi - ki, -128, 128) + 128
            outb[:, :, i * 64:(i + 1) * 64, ap * 64:(ap + 1) * 64] = rel_bias[:, :, idx]
    return outb


def make_in_maps(cfg, inputs, n_batch, halves=2):
    D, E, KD, KE, NH, T, NT = cfg.D, cfg.E, cfg.KD, cfg.KE, cfg.NH, cfg.T, cfg.NT
    x = np.asarray(inputs["x"], np.float32)
    S = x.shape[1]
    own = NT * T
    assert S == halves * own
    pm, pstart = pool_matrices()
    biasn = near_bias(np.asarray(inputs["b_rel_bias"], np.float32), NH)
    shared = {
        "ada_w": np.ascontiguousarray(inputs["ada_w"], np.float32),
        "kv_ada_w": np.ascontiguousarray(inputs["kv_ada_w"], np.float32),
        "a_w_in": np.ascontiguousarray(inputs["a_w_in"], np.float32),
        "a_w_group": np.ascontiguousarray(inputs["a_w_group"], np.float32),
        "a_w_out": np.ascontiguousarray(inputs["a_w_out"], np.float32),
        "w_kv": np.ascontiguousarray(inputs["w_kv"], np.float32),
        "b_w_in": np.ascontiguousarray(inputs["b_w_in"], np.float32),
        "b_w_out": np.ascontiguousarray(inputs["b_w_out"], np.float32),
        "biasn": biasn,
        "pm_std": pm,
        "identm": np.eye(128, dtype=np.float32),
    }
    cf = np.asarray(inputs["b_rel_bias"], np.float32)[:, :, 256].reshape(1, 2 * NH)
    shared["cfar"] = np.ascontiguousarray(np.broadcast_to(cf, (128, 2 * NH)))
    in_maps = []
    for b in range(n_batch):
        vec_rows = [
            np.asarray(inputs["ada_b"], np.float32).reshape(-1, 128),
            np.asarray(inputs["kv_ada_b"], np.float32).reshape(-1, 128),
            np.asarray(inputs["norm_g"], np.float32).reshape(-1, 128),
            np.asarray(inputs["kv_norm_g"], np.float32).reshape(-1, 128),
            np.asarray(inputs["final_g"], np.float32).reshape(-1, 128),
            np.asarray(inputs["a_scale"], np.float32).reshape(-1, 128),
            np.asarray(inputs["c"], np.float32)[b].reshape(-1, 128),
        ]
        vecs = np.ascontiguousarray(np.concatenate(vec_rows, axis=0))
        assert vecs.shape[0] == cfg.NV
        for hf in range(halves):
            t0 = hf * own
            xin = np.zeros((cfg.NTOK, D), np.float32)
            lo = t0 - (128 + T)
            src_lo = max(lo, 0)
            xin[src_lo - lo:, :] = x[b, src_lo:t0 + own, :]
            m = dict(shared)
            m["xin"] = xin
            m["vecs"] = vecs
            pm0 = np.zeros((128, 4, PW + 15), np.float32)
            if t0 == 0:
                pm0[:, :, :PW] = pstart
                hm = np.full((128, 1), NEG, np.float32)
            else:
                pm0[:, :, :PW] = pm
                pm0[:, :, PW:] = pm[:, :, 128:PW]
                hm = np.zeros((128, 1), np.float32)
            m["pm0"] = pm0
            m["hmask"] = hm
            in_maps.append(m)
    return in_maps


_NC_CACHE = {}


def kernel(**inputs):
    cfg = Cfg()
    if "nc" not in _NC_CACHE:
        _NC_CACHE["nc"] = build_program(cfg)
    nc = _NC_CACHE["nc"]
    B = inputs["x"].shape[0]
    in_maps = make_in_maps(cfg, inputs, B)
    res = run_bass_kernel_spmd(nc, in_maps, core_ids=list(range(len(in_maps))))
    own = cfg.NT * cfg.T
    outp = np.empty((B, 2 * own, cfg.D), np.float32)
    i = 0
    for b in range(B):
        for hf in range(2):
            outp[b, hf * own:(hf + 1) * own, :] = res.results[i]["out"]
            i += 1
    return outp
```
